# Optimizing a Trainium2 kernel written in Bass

```python
import jax, jax.numpy as jnp
from jax import lax
import numpy as np

D_MODEL = 1024
BATCH = 16
SEQ = 4096
DEPTH = 2

CHUNK = 64
N_MIXERS = 2
RET_HEADS = 4
RET_DK = D_MODEL // RET_HEADS
RET_DV = 2 * RET_DK
RET_QK = RET_HEADS * RET_DK
RET_VW = RET_HEADS * RET_DV
ROPE_BASE = 10000.0
ATT_HEADS = 16
ATT_DH = D_MODEL // ATT_HEADS
PAST_CHUNKS = 8
BAND_PAST = PAST_CHUNKS * CHUNK
BAND = (PAST_CHUNKS + 1) * CHUNK
MAX_REL = 256
REL_TABLE = MAX_REL + CHUNK
D_FF = 4 * D_MODEL
EPS = 1e-6

kernel_name = "hybrid_retention_chunkattn_encoder"


def _rmsnorm(x, g):
    xf = x.astype(jnp.float32)
    y = xf * lax.rsqrt(jnp.mean(xf * xf, axis=-1, keepdims=True) + EPS)
    return (y * g.astype(jnp.float32)).astype(x.dtype)


def _rope(t, pos):
    d = t.shape[-1]
    half = d // 2
    inv = jnp.exp(-jnp.log(ROPE_BASE) * jnp.arange(half, dtype=jnp.float32) / half)
    ang = pos[:, None] * inv[None, :]
    cos = jnp.cos(ang)[None, :, None, :]
    sin = jnp.sin(ang)[None, :, None, :]
    tf = t.astype(jnp.float32)
    t1, t2 = tf[..., :half], tf[..., half:]
    return jnp.concatenate([t1 * cos - t2 * sin, t1 * sin + t2 * cos], axis=-1).astype(t.dtype)


def _retention(h, w_in, gn_g, w_out):
    B, S, _ = h.shape
    nc = S // CHUNK
    proj = h @ w_in
    q, k, v, g = jnp.split(proj, [RET_QK, 2 * RET_QK, 2 * RET_QK + RET_VW], axis=-1)
    q = q.reshape(B, S, RET_HEADS, RET_DK)
    k = k.reshape(B, S, RET_HEADS, RET_DK)
    v = v.reshape(B, S, RET_HEADS, RET_DV)
    pos = jnp.arange(S, dtype=jnp.float32)
    q = _rope(q, pos)
    k = _rope(k, pos) * (RET_DK ** -0.5)

    def to_chunks(t):
        return t.reshape(B, nc, CHUNK, RET_HEADS, t.shape[-1]).transpose(1, 0, 3, 2, 4)

    qc, kc, vc = to_chunks(q), to_chunks(k), to_chunks(v)
    dt = q.dtype
    log_gamma = jnp.log1p(-jnp.exp2(-5.0 - jnp.arange(RET_HEADS, dtype=jnp.float32)))
    idx = jnp.arange(CHUNK, dtype=jnp.float32)
    intra_decay = jnp.exp(log_gamma[:, None, None] * jnp.abs(idx[:, None] - idx[None, :])).astype(dt)
    key_decay = jnp.exp(log_gamma[:, None] * (CHUNK - 1 - idx)[None, :]).astype(dt)
    query_decay = jnp.exp(log_gamma[:, None] * (idx + 1.0)[None, :]).astype(dt)
    chunk_decay = jnp.exp(log_gamma * CHUNK).astype(dt)

    def step(state, inp):
        qb, kb, vb = inp
        scores = jnp.einsum('bhid,bhjd->bhij', qb, kb) * intra_decay[None]
        o_intra = jnp.einsum('bhij,bhje->bhie', scores, vb)
        o_cross = jnp.einsum('bhid,bhde->bhie', qb, state) * query_decay[None, :, :, None]
        new_state = (state * chunk_decay[None, :, None, None]
                     + jnp.einsum('bhjd,bhje->bhde', kb * key_decay[None, :, :, None], vb))
        return new_state, o_intra + o_cross

    state0 = jnp.zeros((B, RET_HEADS, RET_DK, RET_DV), dt)
    _, o = lax.scan(step, state0, (qc, kc, vc))
    o = o.transpose(1, 0, 3, 2, 4).reshape(B, S, RET_HEADS, RET_DV).astype(jnp.float32)
    mu = jnp.mean(o, axis=-1, keepdims=True)
    var = jnp.mean(jnp.square(o - mu), axis=-1, keepdims=True)
    o = ((o - mu) * lax.rsqrt(var + EPS)).reshape(B, S, RET_VW) * gn_g.astype(jnp.float32)
    y = (jax.nn.silu(g.astype(jnp.float32)) * o).astype(h.dtype)
    return y @ w_out


def _chunk_attention(h, w_in, rel_bias, w_out):
    B, S, _ = h.shape
    nc = S // CHUNK
    q, k, v = jnp.split(h @ w_in, 3, axis=-1)
    q = q.reshape(B, S, ATT_HEADS, ATT_DH)
    k = k.reshape(B, S, ATT_HEADS, ATT_DH)
    v = v.reshape(B, S, ATT_HEADS, ATT_DH)
    pad = ((0, 0), (BAND_PAST, 0), (0, 0), (0, 0))
    kp = jnp.pad(k, pad)
    vp = jnp.pad(v, pad)
    qi = jnp.arange(CHUNK)[:, None]
    kj = jnp.arange(BAND)[None, :]
    rel = kj - BAND_PAST - qi
    bidx = jnp.maximum(rel, -MAX_REL) + MAX_REL
    bias = rel_bias[:, bidx].astype(jnp.float32)
    scale = ATT_DH ** -0.5

    def one_chunk(c):
        start = c * CHUNK
        qb = lax.dynamic_slice_in_dim(q, start, CHUNK, axis=1)
        kb = lax.dynamic_slice_in_dim(kp, start, BAND, axis=1)
        vb = lax.dynamic_slice_in_dim(vp, start, BAND, axis=1)
        s = jnp.einsum('bqhd,bkhd->bhqk', qb, kb).astype(jnp.float32) * scale + bias[None]
        valid = (start - BAND_PAST + jnp.arange(BAND)) >= 0
        s = jnp.where(valid[None, None, None, :], s, -jnp.inf)
        p = jax.nn.softmax(s, axis=-1).astype(vb.dtype)
        return jnp.einsum('bhqk,bkhd->bqhd', p, vb)

    o = lax.map(one_chunk, jnp.arange(nc))
    o = o.transpose(1, 0, 2, 3, 4).reshape(B, S, ATT_HEADS * ATT_DH)
    return o @ w_out


def _sqrelu_mlp(h, w1, w2):
    u = jax.nn.relu(h @ w1)
    return (u * u) @ w2


def setup_inputs(seed: int = 0) -> dict:
    key = jax.random.key(seed)
    ks = jax.random.split(key, 12)
    n_ret = (DEPTH + 1) // 2
    n_att = DEPTH // 2
    f = jnp.float32
    x = jax.random.normal(ks[0], (BATCH, SEQ, D_MODEL), f)
    mix_norm_g = 1.0 + 0.02 * jax.random.normal(ks[1], (DEPTH, D_MODEL), f)
    ret_w_in = jax.random.normal(ks[2], (n_ret, D_MODEL, 2 * RET_QK + 2 * RET_VW), f) * D_MODEL ** -0.5
    ret_gn_g = 1.0 + 0.02 * jax.random.normal(ks[3], (n_ret, RET_VW), f)
    ret_w_out = jax.random.normal(ks[4], (n_ret, RET_VW, D_MODEL), f) * RET_VW ** -0.5
    att_w_in = jax.random.normal(ks[5], (n_att, D_MODEL, 3 * D_MODEL), f) * D_MODEL ** -0.5
    att_rel_bias = 0.1 * jax.random.normal(ks[6], (n_att, ATT_HEADS, REL_TABLE), f)
    att_w_out = jax.random.normal(ks[7], (n_att, D_MODEL, D_MODEL), f) * D_MODEL ** -0.5
    mlp_norm_g = 1.0 + 0.02 * jax.random.normal(ks[8], (DEPTH, D_MODEL), f)
    mlp_w1 = jax.random.normal(ks[9], (DEPTH, D_MODEL, D_FF), f) * D_MODEL ** -0.5
    mlp_w2 = jax.random.normal(ks[10], (DEPTH, D_FF, D_MODEL), f) * D_FF ** -0.5
    final_norm_g = 1.0 + 0.02 * jax.random.normal(ks[11], (D_MODEL,), f)
    return {"x": x, "mix_norm_g": mix_norm_g, "ret_w_in": ret_w_in, "ret_gn_g": ret_gn_g,
            "ret_w_out": ret_w_out, "att_w_in": att_w_in, "att_rel_bias": att_rel_bias,
            "att_w_out": att_w_out, "mlp_norm_g": mlp_norm_g, "mlp_w1": mlp_w1,
            "mlp_w2": mlp_w2, "final_norm_g": final_norm_g}


def reference(x, mix_norm_g, ret_w_in, ret_gn_g, ret_w_out, att_w_in, att_rel_bias,
              att_w_out, mlp_norm_g, mlp_w1, mlp_w2, final_norm_g):
    h = x
    for i in range(DEPTH):
        hn = _rmsnorm(h, mix_norm_g[i])
        j = i // N_MIXERS
        if i % N_MIXERS == 0:
            h = h + _retention(hn, ret_w_in[j], ret_gn_g[j], ret_w_out[j])
        else:
            h = h + _chunk_attention(hn, att_w_in[j], att_rel_bias[j], att_w_out[j])
        h = h + _sqrelu_mlp(_rmsnorm(h, mlp_norm_g[i]), mlp_w1[i], mlp_w2[i])
    return _rmsnorm(h, final_norm_g)
```

```python
import numpy as np
from contextlib import ExitStack
import concourse.bass as bass
import concourse.mybir as mybir
from concourse.bass_utils import run_bass_kernel_spmd

F32 = mybir.dt.float32
BF16 = mybir.dt.bfloat16
AF = mybir.ActivationFunctionType
ALU = mybir.AluOpType

D = 1024
SEQ = 4096
NCORES = 8
EPS = 1e-6
NDS = 32


class Buf:
    __slots__ = ("name", "w", "r", "peers")

    def __init__(self, name=""):
        self.name = name
        self.w = None
        self.r = []
        self.peers = []


class Prog:
    ENG = ("pe", "act", "dve", "pool", "sp")

    def __init__(self, nc, es):
        self.nc = nc
        self.ops = {e: [] for e in self.ENG}
        self.sem = {e: es.enter_context(nc.semaphore("s_" + e)) for e in self.ENG}
        self.dsem = [es.enter_context(nc.semaphore(f"dq{i}")) for i in range(NDS)]
        self.dcnt = [0] * NDS
        self.dnext = 0
        self.pool_hist = []
        self.tag = ""
        self.tags = {e: [] for e in self.ENG}

    def _deps(self, eng, reads, writes):
        deps = []
        for b in reads:
            if b.w is not None:
                deps.append(b.w)
        for b in writes:
            if b.w is not None:
                deps.append(b.w)
            deps.extend(b.r)
            for pb in b.peers:
                if pb.w is not None:
                    deps.append(pb.w)
                deps.extend(pb.r)
        out = []
        seen = set()
        for d in deps:
            if d[0] == "e" and d[1] == eng == "pe":
                continue
            if d not in seen:
                seen.add(d)
                out.append(d)
        return out

    def _commit(self, ev, reads, writes):
        for b in reads:
            b.r.append(ev)
        for b in writes:
            b.w = ev
            b.r = []

    def op(self, eng, fn, reads=(), writes=()):
        deps = self._deps(eng, reads, writes)
        for d in deps:
            if d[0] == "e":
                self.ops[d[1]][d[2]][4] = True
        idx = len(self.ops[eng])
        self.tags[eng].append(self.tag)
        self.ops[eng].append([fn, deps, "op", None, False])
        self._commit(("e", eng, idx), reads, writes)

    def dma(self, eng, out, in_, reads=(), writes=(), **kw):
        deps = self._deps(eng, reads, writes)
        for d in deps:
            if d[0] == "e":
                self.ops[d[1]][d[2]][4] = True
        k = self.dnext
        self.dnext = (self.dnext + 1) % NDS
        if self.dcnt[k] > 0:
            deps.append(("d", k, self.dcnt[k]))
        if eng == "pool" and len(self.pool_hist) >= 2:
            deps.append(self.pool_hist[-2])
        self.dcnt[k] += 16
        ev = ("d", k, self.dcnt[k])
        if eng == "pool":
            self.pool_hist.append(ev)

        def fn(e, out=out, in_=in_, kw=kw):
            return e.dma_start(out=out, in_=in_, **kw)
        self.tags[eng].append(self.tag)
        self.ops[eng].append([fn, deps, "dma", k, False])
        self._commit(ev, reads, writes)
        return ev

    def emit(self, final_events=()):
        nc = self.nc
        pref = {}
        for e in self.ENG:
            c = 0
            arr = []
            for o in self.ops[e]:
                if o[2] == "op" and o[4]:
                    c += 1
                arr.append(c)
            pref[e] = arr

        def run(e, eo):
            seen = {}
            for (fn, deps, kind, k, needs) in self.ops[e]:
                for d in deps:
                    if d[0] == "e":
                        s = self.sem[d[1]]
                        v = pref[d[1]][d[2]]
                    else:
                        s = self.dsem[d[1]]
                        v = d[2]
                    if seen.get(s.num, 0) >= v:
                        continue
                    seen[s.num] = v
                    eo.wait_ge(s, v)
                ins = fn(eo)
                if kind == "dma":
                    ins.then_inc(self.dsem[k], 16)
                elif needs:
                    ins.then_inc(self.sem[e], 1)
            if e == "sp":
                for d in final_events:
                    if d[0] == "d":
                        eo.wait_ge(self.dsem[d[1]], d[2])
                    else:
                        eo.wait_ge(self.sem[d[1]], pref[d[1]][d[2]])

        with nc.Block() as block:
            @block.tensor
            def _(eo):
                run("pe", eo)

            @block.scalar
            def _(eo):
                run("act", eo)

            @block.vector
            def _(eo):
                run("dve", eo)

            @block.gpsimd
            def _(eo):
                run("pool", eo)

            @block.sync
            def _(eo):
                run("sp", eo)


WSPEC = [
    ("win0", 1024, 6144), ("wout0", 2048, 1024), ("w1_0", 1024, 4096), ("w2_0", 4096, 1024),
    ("win1", 1024, 3072), ("wout1", 1024, 1024), ("w1_1", 1024, 4096), ("w2_1", 4096, 1024),
]
WINFO = {}
_off = 0
for _n, _K, _N in WSPEC:
    _KC = _K // 128
    _NW = 4096 // _KC
    WINFO[_n] = (_off, _KC, _NW, _N // _NW)
    _off += _N // _NW
NPIECES = _off


def build(NSEQ, NT, NSC, final_norm=True, stop=99):
    T = NSC * 128
    S = NT * T
    NB = 4 + NSC
    nc = bass.Bass("TRN2", target_bir_lowering=False)

    def din(name, shape, dt=F32):
        return nc.dram_tensor(name, list(shape), dt, kind="ExternalInput").ap()

    x = din("x", [NSEQ, S, D])
    wd = {n: din(n, [K, N]) for n, K, N in WSPEC}
    gains = din("gains", [128, 5, 8])
    gng = din("gng", [1, 2048])
    biasT = din("biasT", [16, 128, 640])
    identf_d = din("identf", [128, 128])
    cosT = din("cosT", [128, SEQ])
    sinT = din("sinT", [128, SEQ])
    maskT_d = din("maskT", [128, 4, 128])
    qd_d = din("qd", [128, 4, 128])
    kd_d = din("kd", [128, 4])
    out = nc.dram_tensor("out", [NSEQ, S, D], F32, kind="ExternalOutput").ap()
    wsc = nc.dram_tensor("wsc", [NPIECES, 128, 4096], BF16, kind="Internal").ap()
    bsc = nc.dram_tensor("bsc", [16, 128, 640], BF16, kind="Internal").ap()

    lg = [np.log1p(-2.0 ** (-5.0 - h)) for h in range(4)]
    cdec = [float(np.exp(lg[h] * 128.0)) for h in range(4)]

    with ExitStack() as es:
        p = Prog(nc, es)

        def sb(name, shape, dt):
            return es.enter_context(nc.sbuf_tensor(name, list(shape), dt))

        h = sb("h", [128, 8, T], F32)
        hB = [Buf(f"h{c}") for c in range(8)]
        hn = sb("hn", [128, 8, T], BF16)
        hnB = [Buf(f"hn{c}") for c in range(8)]
        sq = sb("sq", [128, 8, T], BF16)
        sqB = Buf("sq")
        NSLOT = 4
        wslot = [sb(f"ws{i}", [128, 4096], BF16) for i in range(NSLOT)]
        wslotB = [Buf(f"ws{i}") for i in range(NSLOT)]
        state = sb("state", [128, 8, 512], F32)
        stateB = [Buf(f"st{i}") for i in range(8)]
        state_bf = sb("state_bf", [128, 8, 512], BF16)
        statebB = [Buf(f"stb{i}") for i in range(8)]
        kwin = sb("kwin", [128, 8, NB * 128], BF16)
        kwinB = [[Buf(f"kw{c}_{r}") for r in range(NB)] for c in range(8)]
        vwin = sb("vwin", [128, NB, 1024], BF16)
        vwinB = [[Buf(f"vw{r}_{j}") for j in range(2)] for r in range(NB)]
        gains_sb = sb("gains_sb", [128, 5, 8], F32)
        gnb = sb("gnb", [128, 2048], F32)
        identf = sb("identf_sb", [128, 128], F32)
        identb = sb("identb", [128, 128], BF16)
        ones_bf = sb("ones_bf", [128, 128], BF16)
        maskT = sb("maskT_sb", [128, 4, 128], F32)
        qd = sb("qd_sb", [128, 4, 128], F32)
        kd = sb("kd_sb", [128, 4], F32)
        epst = sb("epst", [128, 1], F32)
        constB = Buf("const")
        cos_sb = [sb(f"cos{i}", [128, T], F32) for i in range(2)]
        sin_sb = [sb(f"sin{i}", [128, T], F32) for i in range(2)]
        ropeB = [Buf(f"rope{i}") for i in range(2)]
        bias_sb = [sb(f"bias{i}", [128, 640], BF16) for i in range(4)]
        biasB = [Buf(f"bias{i}") for i in range(4)]
        io = [sb(f"io{i}", [128, 1024], F32) for i in range(2)]
        ioB = [Buf(f"io{i}") for i in range(2)]
        rstd = sb("rstd", [128, T], F32)
        rstdB = Buf("rstd")
        small = sb("small", [128, 128], F32)
        smallB = [Buf("small0"), Buf("small1")]

        ARENA = 65 * 1024
        arena = sb("arena", [128, ARENA // 2], BF16)
        carved = []

        def carve(phase, off, shape, dt, nbufs=1, name=""):
            esz = 4 if dt == F32 else 2
            n = int(np.prod(shape))
            nbytes = n * esz
            assert off % 4 == 0 and off + nbytes <= ARENA, (name, off, nbytes)
            ap = arena[:, off // 2: (off + nbytes) // 2]
            if dt == F32:
                ap = ap.bitcast(F32)
            if len(shape) == 2:
                ap = ap.rearrange("p (a b) -> p a b", a=shape[0])
            elif len(shape) == 3:
                ap = ap.rearrange("p (a b c) -> p a b c", a=shape[0], b=shape[1])
            bufs = [Buf(f"{name}{i}") for i in range(nbufs)]
            for (ph, s0, e0, bs) in carved:
                if ph != phase and s0 < off + nbytes and off < e0:
                    for b in bufs:
                        b.peers.extend(bs)
                    for b in bs:
                        b.peers.extend(bufs)
            carved.append((phase, off, off + nbytes, bufs))
            return ap, bufs, off + nbytes

        o_ = 0
        qT, qTB, o_ = carve("L0", o_, [8, T], BF16, 8, "qT")
        kT, kTB, o_ = carve("L0", o_, [8, T], BF16, 8, "kT")
        v0, v0B, o_ = carve("L0", o_, [NSC, 2048], BF16, NSC * 4, "v0")
        sg, sgB, o_ = carve("L0", o_, [NSC, 2048], BF16, NSC * 4, "sg")
        yT, yTB, o_ = carve("L0", o_, [16, T], BF16, NSC * 2, "yT")
        qdec, qdecB, o_ = carve("L0", o_, [2, 8, 128], BF16, 2, "qdec")
        kdec, kdecB, o_ = carve("L0", o_, [2, 1024], BF16, 2, "kdec")
        scb, scbB, o_ = carve("L0", o_, [2, 512], BF16, 2, "scb")
        ytm, ytmB, o_ = carve("L0", o_, [2, 2048], BF16, 8, "ytm")
        rtmp, rtmpB, o_ = carve("L0", o_, [4, T], F32, 4, "rtmp")
        onb, onbB, o_ = carve("L0", o_, [2, 512], F32, 2, "onb")
        gtmp, gtmpB, o_ = carve("L0", o_, [2, 512], F32, 2, "gtmp")
        crs, crsB, o_ = carve("L0", o_, [2, T], F32, 2, "crs")
        L0_END = o_
        o_ = 0
        uT, uTB, o_ = carve("MLP", o_, [32, T], BF16, 32, "uT")
        rl, rlB, o_ = carve("MLP", o_, [3, T], F32, 3, "rl")
        o_ = 0
        qm, qmB, o_ = carve("L1", o_, [16, T], BF16, 16, "qm")
        oT1, oT1B, o_ = carve("L1", o_, [8, T], BF16, 8, "oT1")
        PT, PTB, o_ = carve("L1", o_, [2, NB, T], BF16, 2 * NB, "PT")
        rec, recB, o_ = carve("L1", o_, [2, T], F32, 2, "rec")
        o_ = 8192
        fin, finB, o_ = carve("FIN", o_, [8, T], F32, 8, "fin")

        banks = [es.enter_context(nc.psum_tensor(f"pb{i}", [128, 512], F32)) for i in range(8)]
        bankB = [Buf(f"pb{i}") for i in range(8)]
        bstate = {"i": 0}

        bankH = [[Buf(f"pb{i}h{j}") for j in range(2)] for i in range(8)]
        for i in range(8):
            for hb_ in bankH[i]:
                hb_.peers.append(bankB[i])
                bankB[i].peers.append(hb_)

        held = set()

        def bank(hold=False):
            i = bstate["i"]
            while i in held:
                i = (i + 1) % 8
            bstate["i"] = (i + 1) % 8
            if hold:
                held.add(i)
            return banks[i], bankB[i]

        def release(bkB):
            held.discard(bankB.index(bkB))

        def bank_h():
            i = bstate["i"]
            bstate["i"] = (i + 1) % 8
            return banks[i], bankH[i]

        pend = {"fn": None, "cnt": 0}

        def mm(outap, lhsT, rhs, start, stop, reads, writes):
            p.op("pe", lambda e, o=outap, l=lhsT, r=rhs, s=start, t=stop:
                 e.matmul(o, lhsT=l, rhs=r, start=s, stop=t), reads=reads, writes=writes)
            if pend["fn"] is not None:
                pend["cnt"] += 1
                if pend["cnt"] >= 8:
                    f = pend["fn"]
                    pend["fn"] = None
                    f()

        def tr(outap, inap, ident, reads, writes):
            p.op("pe", lambda e, o=outap, i=inap, d=ident: e.transpose(o, i, d), reads=reads, writes=writes)

        wpieceB = [Buf(f"wp{i}") for i in range(NPIECES)]
        wst = {"slot": 0}

        wst["inline"] = False
        wst["q"] = 0

        def wload(name, j):
            off, KC, NW, npc = WINFO[name]
            pi = off + j
            s = wst["slot"]
            wst["slot"] = (s + 1) % NSLOT
            if not wst["inline"]:
                p.dma("sp", wslot[s][:, :], wsc[pi, :, :], reads=[wpieceB[pi]], writes=[wslotB[s]])
            else:
                src = wd[name].rearrange("(kc p) n -> p kc n", p=128)
                kq = KC // 4
                for qq in range(4):
                    qi = wst["q"] % 4
                    wst["q"] += 1
                    stg, stgB = stage4[qi]
                    p.dma("sp", stg[:, :].rearrange("p (k n) -> p k n", k=kq),
                          src[:, qq * kq:(qq + 1) * kq, j * NW:(j + 1) * NW], writes=[stgB])
                    dst = wslot[s][:, qq * 1024:(qq + 1) * 1024]
                    if qq % 2 == 0:
                        p.op("act", lambda e, dst=dst, stg=stg: e.activation(out=dst, in_=stg[:, :], func=AF.Copy),
                             reads=[stgB], writes=[wslotB[s]])
                    else:
                        p.op("dve", lambda e, dst=dst, stg=stg: e.tensor_copy(out=dst, in_=stg[:, :]),
                             reads=[stgB], writes=[wslotB[s]])
                p.dma("act", wsc[pi, :, :], wslot[s][:, :], reads=[wslotB[s]], writes=[wpieceB[pi]])
            return wslot[s][:, :].rearrange("p (k n) -> p k n", k=KC), wslotB[s]

        p.dma("sp", gains_sb[:], gains[:, :, :], writes=[constB])
        p.dma("sp", gnb[:], gng.partition_broadcast(128) if False else gng[0:1, :].broadcast_to([128, 2048]), writes=[constB])
        p.dma("sp", identf[:], identf_d[:, :], writes=[constB])
        p.dma("sp", maskT[:], maskT_d[:, :, :], writes=[constB])
        p.dma("sp", qd[:], qd_d[:, :, :], writes=[constB])
        p.dma("sp", kd[:], kd_d[:, :], writes=[constB])
        p.op("dve", lambda e: e.tensor_copy(out=identb[:], in_=identf[:]), reads=[constB], writes=[constB])
        p.op("dve", lambda e: e.memset(ones_bf[:], 1.0), writes=[constB])
        p.op("dve", lambda e: e.memset(epst[:], EPS), writes=[constB])
        stg32, stg32B, o2_ = carve("PRO", 0, [2, 4096], F32, 2, "stg32")
        stgbf, stgbfB, o2_ = carve("PRO", o2_, [2, 4096], BF16, 2, "stgbf")
        pi_ = 0
        bscB = Buf("bsc")
        for hd in range(16):
            b = pi_ % 2
            p.dma("sp", stg32[:, b, 0:640], biasT[hd, :, :], writes=[stg32B[b]])
            p.op("dve", lambda e, b=b: e.tensor_copy(out=stgbf[:, b, 0:640], in_=stg32[:, b, 0:640]),
                 reads=[stg32B[b]], writes=[stgbfB[b]])
            p.dma("act", bsc[hd, :, :], stgbf[:, b, 0:640], reads=[stgbfB[b]], writes=[bscB])
            pi_ += 1

        rcol = sb("rcol", [128, 4], F32)
        rcolB = Buf("rcol")
        rs2 = sb("rs2", [128, T], F32)
        rs2B = Buf("rs2")

        def rmsnorm(gi, out_fin=False, need_col=False, need_sq=False, after=None):
            for c in range(8):
                dst = fin if out_fin else hn
                dstB = finB if out_fin else hnB
                if c % 2 == 0:
                    p.op("act", lambda e, c=c, dst=dst: e.activation(out=dst[:, c, :], in_=h[:, c, :], func=AF.Identity,
                                                                     scale=gains_sb[:, gi, c:c + 1]),
                         reads=[hB[c], constB], writes=[dstB[c]])
                else:
                    p.op("dve", lambda e, c=c, dst=dst: e.tensor_scalar(out=dst[:, c, :], in0=h[:, c, :],
                                                                        scalar1=gains_sb[:, gi, c:c + 1], scalar2=None,
                                                                        op0=ALU.mult),
                         reads=[hB[c], constB], writes=[dstB[c]])
            p.op("act", lambda e: e.activation(out=sq[:], in_=h[:], func=AF.Square),
                 reads=hB, writes=[sqB])

            def stats():
                bk, bkB = bank()
                for c in range(8):
                    mm(bk[:, 0:T], ones_bf[:, :], sq[:, c, :], c == 0, c == 7, [sqB, constB], [bkB])
                if need_col:
                    bc, bcB = bank()
                    for sc in range(NSC):
                        for c in range(8):
                            mm(bc[:, sc:sc + 1], sq[:, c, sc * 128:(sc + 1) * 128], ones_bf[:, 0:1], c == 0, c == 7,
                               [sqB, constB], [bcB])
                p.op("act", lambda e, bk=bk: e.activation(out=rstd[:], in_=bk[:, 0:T], func=AF.Sqrt,
                                                          scale=1.0 / D, bias=epst[:, 0:1]),
                     reads=[bkB, constB], writes=[rstdB])
                p.op("dve", lambda e: e.reciprocal(out=rstd[:], in_=rstd[:]), reads=[rstdB], writes=[rstdB])
                if need_col:
                    p.op("act", lambda e, bc=bc: e.activation(out=rcol[:, 0:NSC], in_=bc[:, 0:NSC], func=AF.Sqrt,
                                                              scale=1.0 / D, bias=epst[:, 0:1]),
                         reads=[bcB, constB], writes=[rcolB])
                    p.op("dve", lambda e: e.reciprocal(out=rcol[:, 0:NSC], in_=rcol[:, 0:NSC]), reads=[rcolB], writes=[rcolB])
                if need_sq:
                    p.op("dve", lambda e: e.tensor_tensor(out=rs2[:], in0=rstd[:], in1=rstd[:], op=ALU.mult),
                         reads=[rstdB], writes=[rs2B])
                if out_fin:
                    for c in range(8):
                        p.op("dve", lambda e, c=c: e.tensor_tensor(
                            out=fin[:, c, :], in0=fin[:, c, :], in1=rstd[:], op=ALU.mult),
                            reads=[rstdB, finB[c]], writes=[finB[c]])
                if after is not None:
                    after()

            if out_fin:
                stats()
            else:
                assert pend["fn"] is None
                pend["fn"] = stats
                pend["cnt"] = 0

        def mlp(layer, gi):
            rmsnorm(gi, need_sq=True)
            n1, n2 = f"w1_{layer}", f"w2_{layer}"
            for j in range(8):
                wp, wB = wload(n1, j)
                for q in range(4):
                    fc = j * 4 + q
                    bk, bkB = bank()
                    for kc in range(8):
                        mm(bk[:, 0:T], wp[:, kc, q * 128:(q + 1) * 128], hn[:, kc, :], kc == 0, kc == 7,
                           [wB, hnB[kc]], [bkB])
                    r = fc % 3
                    p.op("act", lambda e, bk=bk, r=r: e.activation(out=rl[:, r, :], in_=bk[:, 0:T], func=AF.Relu),
                         reads=[bkB], writes=[rlB[r]])
                    p.op("dve", lambda e, r=r, fc=fc: e.tensor_tensor(out=uT[:, fc, :], in0=rl[:, r, :],
                                                                      in1=rl[:, r, :], op=ALU.mult),
                         reads=[rlB[r]], writes=[uTB[fc]])
            for j in range(8):
                wp, wB = wload(n2, j)
                bk, bkB = bank()
                for fc in range(32):
                    mm(bk[:, 0:T], wp[:, fc, :], uT[:, fc, :], fc == 0, fc == 31, [wB, uTB[fc]], [bkB])
                r = j % 3
                p.op("dve", lambda e, bk=bk, r=r: e.tensor_tensor(out=rl[:, r, :], in0=bk[:, 0:T], in1=rs2[:],
                                                                  op=ALU.mult), reads=[bkB, rs2B], writes=[rlB[r]])
                p.op("dve", lambda e, j=j, r=r: e.tensor_tensor(out=h[:, j, :], in0=h[:, j, :], in1=rl[:, r, :],
                                                                op=ALU.add), reads=[rlB[r], hB[j]], writes=[hB[j]])

        out_events = []
        xin = [sb(f"xin{i}", [128, 1024], F32) for i in range(2)]
        xinB = [Buf(f"xin{i}") for i in range(2)]
        NBIAS = 4
        stage4 = [(io[0], ioB[0]), (io[1], ioB[1]), (xin[0], xinB[0]), (xin[1], xinB[1])]

        def xin_load(s, t):
            t0 = t * T
            for sc in range(NSC):
                p.dma("sp", xin[sc % 2][:, :], x[s, t0 + sc * 128: t0 + (sc + 1) * 128, :], writes=[xinB[sc % 2]])

        def xin_tr(s, t):
            p.tag = "xin"
            for sc in range(NSC):
                ib = sc % 2
                for g in range(2):
                    bk, bkB = bank()
                    for q in range(4):
                        c = g * 4 + q
                        tr(bk[:, q * 128:(q + 1) * 128], xin[ib][:, c * 128:(c + 1) * 128], identf[:, :],
                           [xinB[ib], constB], [bkB])
                    if g == 0:
                        p.op("act", lambda e, bk=bk, g=g, sc=sc: e.activation(
                            out=h[:, g * 4:(g + 1) * 4, sc * 128:(sc + 1) * 128],
                            in_=bk[:, :].rearrange("p (a b) -> p a b", a=4), func=AF.Copy),
                            reads=[bkB], writes=hB[g * 4:(g + 1) * 4])
                    else:
                        p.op("dve", lambda e, bk=bk, g=g, sc=sc: e.tensor_copy(
                            out=h[:, g * 4:(g + 1) * 4, sc * 128:(sc + 1) * 128],
                            in_=bk[:, :].rearrange("p (a b) -> p a b", a=4)),
                            reads=[bkB], writes=hB[g * 4:(g + 1) * 4])

        def layer0_a(s, t):
            t0 = t * T
            par = t % 2
            p.dma("sp", cos_sb[par][:, :], cosT[:, t0:t0 + T], writes=[ropeB[par]])
            p.dma("sp", sin_sb[par][:, :], sinT[:, t0:t0 + T], writes=[ropeB[par]])
            p.tag = "L0.norm"
            def scaled_tables():
                p.op("dve", lambda e: e.tensor_tensor(out=crs[:, 0, :], in0=cos_sb[par][:, :], in1=rstd[:], op=ALU.mult),
                     reads=[ropeB[par], rstdB], writes=[crsB[0]])
                p.op("dve", lambda e: e.tensor_tensor(out=crs[:, 1, :], in0=sin_sb[par][:, :], in1=rstd[:], op=ALU.mult),
                     reads=[ropeB[par], rstdB], writes=[crsB[1]])

            rmsnorm(0, need_col=True, after=scaled_tables)
            p.tag = "L0.qk"
            for pj in range(4):
                wp, wB = wload("win0", pj)
                for hp in range(2):
                    cpair = pj * 4 + hp * 2
                    bks = []
                    for dc in range(2):
                        bk, bkB = bank()
                        q = hp * 2 + dc
                        for kc in range(8):
                            mm(bk[:, 0:T], wp[:, kc, q * 128:(q + 1) * 128], hn[:, kc, :], kc == 0, kc == 7,
                               [wB, hnB[kc]], [bkB])
                        bks.append((bk, bkB))
                    (b1, b1B), (b2, b2B) = bks
                    isq = cpair < 8
                    dst = qT if isq else kT
                    dstB = qTB if isq else kTB
                    c1 = cpair % 8
                    cs, sn = crs[:, 0, :], crs[:, 1, :]
                    p.op("dve", lambda e, b1=b1, cs=cs: e.tensor_tensor(out=rtmp[:, 0, :], in0=b1[:, 0:T], in1=cs, op=ALU.mult),
                         reads=[b1B, crsB[0]], writes=[rtmpB[0]])
                    p.op("dve", lambda e, b2=b2, sn=sn: e.tensor_tensor(out=rtmp[:, 1, :], in0=b2[:, 0:T], in1=sn, op=ALU.mult),
                         reads=[b2B, crsB[1]], writes=[rtmpB[1]])
                    p.op("dve", lambda e, b1=b1, sn=sn: e.tensor_tensor(out=rtmp[:, 2, :], in0=b1[:, 0:T], in1=sn, op=ALU.mult),
                         reads=[b1B, crsB[1]], writes=[rtmpB[2]])
                    p.op("dve", lambda e, b2=b2, cs=cs: e.tensor_tensor(out=rtmp[:, 3, :], in0=b2[:, 0:T], in1=cs, op=ALU.mult),
                         reads=[b2B, crsB[0]], writes=[rtmpB[3]])
                    p.op("pool", lambda e, dst=dst, c1=c1: e.tensor_tensor(out=dst[:, c1, :], in0=rtmp[:, 0, :], in1=rtmp[:, 1, :], op=ALU.subtract),
                         reads=[rtmpB[0], rtmpB[1]], writes=[dstB[c1]])
                    p.op("pool", lambda e, dst=dst, c1=c1: e.tensor_tensor(out=dst[:, c1 + 1, :], in0=rtmp[:, 2, :], in1=rtmp[:, 3, :], op=ALU.add),
                         reads=[rtmpB[2], rtmpB[3]], writes=[dstB[c1 + 1]])

        def layer0_b(s, t):
            p.tag = "L0.vg"
            for pj in range(8):
                wp, wB = wload("win0", 4 + pj)
                for sc in range(NSC):
                    bk, bkB = bank()
                    for kc in range(8):
                        mm(bk[:, :], hn[:, kc, sc * 128:(sc + 1) * 128], wp[:, kc, :], kc == 0, kc == 7,
                           [wB, hnB[kc]], [bkB])
                    if pj < 4:
                        p.op("act", lambda e, bk=bk, sc=sc, pj=pj: e.activation(
                            out=v0[:, sc, pj * 512:(pj + 1) * 512], in_=bk[:, :], func=AF.Identity,
                            scale=rcol[:, sc:sc + 1]),
                            reads=[bkB, rcolB], writes=[v0B[sc * 4 + pj]])
                    else:
                        g4 = pj - 4
                        gi_ = (sc + pj) % 2
                        p.op("act", lambda e, bk=bk, gi_=gi_, sc=sc: e.activation(
                            out=gtmp[:, gi_, :], in_=bk[:, :], func=AF.Silu, scale=rcol[:, sc:sc + 1]),
                            reads=[bkB, rcolB], writes=[gtmpB[gi_]])
                        p.op("dve", lambda e, sc=sc, g4=g4, gi_=gi_: e.tensor_tensor(
                            out=sg[:, sc, g4 * 512:(g4 + 1) * 512], in0=gtmp[:, gi_, :],
                            in1=gnb[:, g4 * 512:(g4 + 1) * 512], op=ALU.mult),
                            reads=[gtmpB[gi_], constB], writes=[sgB[sc * 4 + g4]])
            p.tag = "L0.ret"

            def ytr(sc):
                pb = sc % 2
                cols = slice(sc * 128, (sc + 1) * 128)
                for g in range(2):
                    bk, bkB = bank()
                    bkb = bk[:, :].bitcast(BF16)
                    for q in range(8):
                        ec = g * 8 + q
                        tr(bkb[:, q * 128:(q + 1) * 128], ytm[:, pb, ec * 128:(ec + 1) * 128], identb[:, :],
                           [ytmB[pb * 4 + ec // 4], constB], [bkB])
                    p.op("act", lambda e, bkb=bkb, g=g, cols=cols: e.activation(
                        out=yT[:, g * 8:(g + 1) * 8, cols], in_=bkb[:, :].rearrange("p (a b) -> p a b", a=8),
                        func=AF.Copy), reads=[bkB], writes=[yTB[sc * 2 + g]])

            for sc in range(NSC):
                first = (t == 0 and sc == 0)
                last = (t == NT - 1 and sc == NSC - 1)
                pb = sc % 2
                cols = slice(sc * 128, (sc + 1) * 128)
                for c in range(8):
                    p.op("pool", lambda e, c=c, pb=pb, cols=cols: e.tensor_tensor(
                        out=qdec[:, pb, c, :], in0=qT[:, c, cols], in1=qd[:, c // 2, :], op=ALU.mult),
                        reads=[qTB[c], constB], writes=[qdecB[pb]])
                bk, bkB = bank()
                bkb = bk[:, :].bitcast(BF16)
                for c in range(8):
                    tr(bkb[:, c * 128:(c + 1) * 128], kT[:, c, cols], identb[:, :], [kTB[c], constB], [bkB])
                for hh in range(4):
                    p.op("act", lambda e, bkb=bkb, hh=hh, pb=pb: e.activation(
                        out=kdec[:, pb, hh * 256:(hh + 1) * 256], in_=bkb[:, hh * 256:(hh + 1) * 256],
                        func=AF.Identity, scale=kd[:, hh:hh + 1]),
                        reads=[bkB, constB], writes=[kdecB[pb]])
                bs, bsB = bank()
                for hh in range(4):
                    for dc in range(2):
                        c = 2 * hh + dc
                        mm(bs[:, hh * 128:(hh + 1) * 128], kT[:, c, cols], qT[:, c, cols], dc == 0, dc == 1,
                           [kTB[c], qTB[c]], [bsB])
                p.op("dve", lambda e, bs=bs, pb=pb: e.tensor_tensor(
                    out=scb[:, pb, :], in0=bs[:, :], in1=maskT[:, :, :].rearrange("p a b -> p (a b)"), op=ALU.mult),
                    reads=[bsB, constB], writes=[scbB[pb]])
                obanks = []
                for hh in range(4):
                    bo, boB = bank(hold=True)
                    obanks.append((bo, boB))
                    mm(bo[:, :], scb[:, pb, hh * 128:(hh + 1) * 128], v0[:, sc, hh * 512:(hh + 1) * 512],
                       True, first, [scbB[pb], v0B[sc * 4 + hh]], [boB])
                    if not first:
                        for dc in range(2):
                            c = 2 * hh + dc
                            mm(bo[:, :], qdec[:, pb, c, :], state_bf[:, c, :], False, dc == 1,
                               [qdecB[pb], statebB[c]], [boB])
                if sc > 0:
                    ytr(sc - 1)
                sm3 = small[:, pb * 64:(pb + 1) * 64].rearrange("p (a b) -> p a b", a=4)
                smB = smallB[pb]
                for hh in range(4):
                    bo, boB = obanks[hh]
                    p.op("dve", lambda e, bo=bo, hh=hh, sm3=sm3: e.bn_stats(out=sm3[:, hh, 0:6], in_=bo[:, :]),
                         reads=[boB], writes=[smB])
                for hh in range(4):
                    p.op("dve", lambda e, hh=hh, sm3=sm3: e.bn_aggr(out=sm3[:, hh, 6:8], in_=sm3[:, hh, 0:6]),
                         reads=[smB], writes=[smB])
                p.op("act", lambda e, sm3=sm3: e.activation(out=sm3[:, :, 8:9], in_=sm3[:, :, 7:8], func=AF.Sqrt,
                                                          bias=epst[:, 0:1], scale=1.0),
                     reads=[smB, constB], writes=[smB])

                def state_upd(clist):
                    for c in clist:
                        hh = c // 2
                        bu, buB = bank()
                        mm(bu[:, :], kdec[:, pb, c * 128:(c + 1) * 128], v0[:, sc, hh * 512:(hh + 1) * 512],
                           True, True, [kdecB[pb], v0B[sc * 4 + hh]], [buB])
                        if first:
                            p.op("dve", lambda e, bu=bu, c=c: e.tensor_copy(out=state[:, c, :], in_=bu[:, :]),
                                 reads=[buB], writes=[stateB[c]])
                        else:
                            p.op("dve", lambda e, bu=bu, c=c, hh=hh: e.scalar_tensor_tensor(
                                out=state[:, c, :], in0=state[:, c, :], scalar=cdec[hh], in1=bu[:, :],
                                op0=ALU.mult, op1=ALU.add), reads=[buB, stateB[c]], writes=[stateB[c]])
                        if sc == NSC - 1:
                            p.op("pool", lambda e, c=c: e.tensor_copy(out=state_bf[:, c, :], in_=state[:, c, :]),
                                 reads=[stateB[c]], writes=[statebB[c]])
                        else:
                            p.op("act", lambda e, c=c: e.activation(out=state_bf[:, c, :], in_=state[:, c, :], func=AF.Copy),
                                 reads=[stateB[c]], writes=[statebB[c]])

                if not last:
                    state_upd(range(0, 4))
                p.op("dve", lambda e, sm3=sm3: e.reciprocal(out=sm3[:, :, 9:10], in_=sm3[:, :, 8:9]),
                     reads=[smB], writes=[smB])
                p.op("dve", lambda e, sm3=sm3: e.scalar_tensor_tensor(
                    out=sm3[:, :, 10:11], in0=sm3[:, :, 6:7], scalar=-1.0, in1=sm3[:, :, 9:10], op0=ALU.mult, op1=ALU.mult),
                    reads=[smB], writes=[smB])
                for hh in range(4):
                    bo, boB = obanks[hh]
                    ob = hh % 2
                    p.op("act", lambda e, bo=bo, sm3=sm3, ob=ob, hh=hh: e.activation(
                        out=onb[:, ob, :], in_=bo[:, :], func=AF.Identity, scale=sm3[:, hh, 9:10], bias=sm3[:, hh, 10:11]),
                        reads=[boB, smB], writes=[onbB[ob]])
                    p.op("dve", lambda e, ob=ob, pb=pb, hh=hh, sc=sc: e.tensor_tensor(
                        out=ytm[:, pb, hh * 512:(hh + 1) * 512], in0=onb[:, ob, :],
                        in1=sg[:, sc, hh * 512:(hh + 1) * 512], op=ALU.mult),
                        reads=[onbB[ob], sgB[sc * 4 + hh]], writes=[ytmB[pb * 4 + hh]])
                    release(boB)
                if not last:
                    state_upd(range(4, 8))
            ytr(NSC - 1)
            p.tag = "L0.wout"
            for pj in range(4):
                wp, wB = wload("wout0", pj)
                for q in range(2):
                    ncn = pj * 2 + q
                    bk, bkB = bank()
                    for ec in range(16):
                        mm(bk[:, 0:T], wp[:, ec, q * 128:(q + 1) * 128], yT[:, ec, :], ec == 0, ec == 15,
                           [wB] + yTB, [bkB])
                    p.op("dve", lambda e, bk=bk, ncn=ncn: e.tensor_tensor(
                        out=h[:, ncn, :], in0=h[:, ncn, :], in1=bk[:, 0:T], op=ALU.add),
                        reads=[bkB, hB[ncn]], writes=[hB[ncn]])

        def layer1(s, t):
            p.tag = "L1.qkv"
            rmsnorm(2, need_col=True)
            A0 = t * NSC
            p.op("pool", lambda e: e.memset(qm[:], 0.0), writes=qmB)
            for pj in range(4):
                wp, wB = wload("win1", pj)
                for q in range(4):
                    c = (pj % 2) * 4 + q
                    bk, bkB = bank()
                    for kc in range(8):
                        mm(bk[:, 0:T], wp[:, kc, q * 128:(q + 1) * 128], hn[:, kc, :], kc == 0, kc == 7,
                           [wB, hnB[kc]], [bkB])
                    if pj < 2:
                        p.op("dve", lambda e, bk=bk, c=c: e.scalar_tensor_tensor(
                            out=qm[0:64, 2 * c, :], in0=bk[0:64, 0:T], scalar=0.125, in1=rstd[0:64, :],
                            op0=ALU.mult, op1=ALU.mult), reads=[bkB, rstdB], writes=[qmB[2 * c]])
                        p.op("dve", lambda e, bk=bk, c=c: e.scalar_tensor_tensor(
                            out=qm[64:128, 2 * c + 1, :], in0=bk[64:128, 0:T], scalar=0.125, in1=rstd[64:128, :],
                            op0=ALU.mult, op1=ALU.mult), reads=[bkB, rstdB], writes=[qmB[2 * c + 1]])
                    else:
                        for sc in range(NSC):
                            r = (A0 + sc) % NB
                            p.op("dve", lambda e, bk=bk, c=c, r=r, sc=sc: e.tensor_tensor(
                                out=kwin[:, c, r * 128:(r + 1) * 128], in0=bk[:, sc * 128:(sc + 1) * 128],
                                in1=rstd[:, sc * 128:(sc + 1) * 128], op=ALU.mult),
                                reads=[bkB, rstdB], writes=[kwinB[c][r]])
            for pj in range(2):
                wp, wB = wload("win1", 4 + pj)
                for sc in range(NSC):
                    r = (A0 + sc) % NB
                    bk, bkB = bank()
                    for kc in range(8):
                        mm(bk[:, :], hn[:, kc, sc * 128:(sc + 1) * 128], wp[:, kc, :], kc == 0, kc == 7,
                           [wB, hnB[kc]], [bkB])
                    p.op("act", lambda e, bk=bk, r=r, pj=pj, sc=sc: e.activation(
                        out=vwin[:, r, pj * 512:(pj + 1) * 512], in_=bk[:, :], func=AF.Identity,
                        scale=rcol[:, sc:sc + 1]),
                        reads=[bkB, rcolB], writes=[vwinB[r][pj]])
            KB0 = max(0, A0 - 4)
            KBs = list(range(KB0, A0 + NSC))
            p.tag = "L1.attn"
            wout1_pre = [wload("wout1", pj) for pj in range(2)]
            ent = []
            for KB in KBs:
                a_lo = max(KB, A0)
                a_hi = min(KB + 4, A0 + NSC - 1)
                if a_lo > a_hi:
                    continue
                ent.append((KB, (a_lo - A0) * 128, (a_hi - A0 + 1) * 128, a_lo - KB))

            def bias_load(hd):
                hb4 = hd % NBIAS
                p.dma("sp", bias_sb[hb4][:, :], bsc[hd, :, :], reads=[bscB], writes=[biasB[hb4]])

            def attn_a(hd):
                pc = hd // 2
                hb = hd % 2
                hb4 = hd % NBIAS
                if hd + NBIAS - 1 < 16:
                    bias_load(hd + NBIAS - 1)
                per_bank = 512 // T
                for g0 in range(0, len(ent), per_bank):
                    grp = ent[g0:g0 + per_bank]
                    bk, bkB = bank()
                    for gi_, (KB, qlo, qhi, d_lo) in enumerate(grp):
                        off = gi_ * T
                        r = KB % NB
                        mm(bk[:, off + qlo:off + qhi], kwin[:, pc, r * 128:(r + 1) * 128], qm[:, hd, qlo:qhi],
                           True, False, [kwinB[pc][r], qmB[hd]], [bkB])
                        mm(bk[:, off + qlo:off + qhi], identb[:, :],
                           bias_sb[hb4][:, d_lo * 128: d_lo * 128 + (qhi - qlo)], False, True, [constB, biasB[hb4]], [bkB])
                    for gi_, (KB, qlo, qhi, d_lo) in enumerate(grp):
                        off = gi_ * T
                        ki = KB - KB0
                        p.op("act", lambda e, bk=bk, off=off, qlo=qlo, qhi=qhi, hb=hb, ki=ki: e.activation(
                            out=PT[:, hb, ki, qlo:qhi], in_=bk[:, off + qlo:off + qhi], func=AF.Exp),
                            reads=[bkB], writes=[PTB[hb * NB + ki]])

            pair_banks = {}

            def attn_b(hd):
                pc, half = hd // 2, hd % 2
                prt = slice(half * 64, half * 64 + 64)
                hb = hd % 2
                if half == 0:
                    pair_banks[pc] = (bank(), bank())
                (bo, boB), (bsm, bsmB) = pair_banks[pc]
                for i, (KB, qlo, qhi, d_lo) in enumerate(ent):
                    r = KB % NB
                    ki = KB - KB0
                    p.op("pe", lambda e, bo=bo, prt=prt, qlo=qlo, qhi=qhi, r=r, hd=hd, hb=hb, ki=ki, i=i: e.matmul(
                        bo[prt, qlo:qhi], lhsT=vwin[:, r, hd * 64:(hd + 1) * 64], rhs=PT[:, hb, ki, qlo:qhi],
                        start=(i == 0), stop=(i == len(ent) - 1), skip_group_check=True),
                        reads=[vwinB[r][hd // 8], PTB[hb * NB + ki]], writes=[boB])
                for i, (KB, qlo, qhi, d_lo) in enumerate(ent):
                    ki = KB - KB0
                    p.op("pe", lambda e, bsm=bsm, prt=prt, qlo=qlo, qhi=qhi, hb=hb, ki=ki, i=i: e.matmul(
                        bsm[prt, qlo:qhi], lhsT=ones_bf[:, 0:64], rhs=PT[:, hb, ki, qlo:qhi],
                        start=(i == 0), stop=(i == len(ent) - 1), skip_group_check=True),
                        reads=[constB, PTB[hb * NB + ki]], writes=[bsmB])
                if half == 1:
                    rb = pc % 2
                    p.op("dve", lambda e, bsm=bsm, rb=rb: e.reciprocal(out=rec[:, rb, :], in_=bsm[:, 0:T]),
                         reads=[bsmB], writes=[recB[rb]])
                    p.op("dve", lambda e, bo=bo, rb=rb, pc=pc: e.tensor_tensor(
                        out=oT1[:, pc, :], in0=bo[:, 0:T], in1=rec[:, rb, :], op=ALU.mult),
                        reads=[boB, recB[rb]], writes=[oT1B[pc]])

            for hd in range(min(NBIAS - 1, 16)):
                bias_load(hd)
            attn_a(0)
            for hd in range(16):
                if hd + 1 < 16:
                    attn_a(hd + 1)
                attn_b(hd)
            p.tag = "L1.wout"
            for pj in range(2):
                wp, wB = wout1_pre[pj]
                for q in range(4):
                    ncn = pj * 4 + q
                    bk, bkB = bank()
                    for kc in range(8):
                        mm(bk[:, 0:T], wp[:, kc, q * 128:(q + 1) * 128], oT1[:, kc, :], kc == 0, kc == 7,
                           [wB, oT1B[kc]], [bkB])
                    p.op("dve", lambda e, bk=bk, ncn=ncn: e.tensor_tensor(
                        out=h[:, ncn, :], in0=h[:, ncn, :], in1=bk[:, 0:T], op=ALU.add),
                        reads=[bkB, hB[ncn]], writes=[hB[ncn]])

        def fin_norm():
            p.tag = "fin"
            if final_norm:
                rmsnorm(4, out_fin=True)
            else:
                for c in range(8):
                    p.op("dve", lambda e, c=c: e.tensor_copy(out=fin[:, c, :], in_=h[:, c, :]),
                         reads=[hB[c]], writes=[finB[c]])

        def fin_out(s, t):
            p.tag = "fin"
            t0 = t * T
            for sc in range(NSC):
                ib = sc % 2
                for g in range(2):
                    bk, bkB = bank()
                    for q in range(4):
                        c = g * 4 + q
                        tr(bk[:, q * 128:(q + 1) * 128], fin[:, c, sc * 128:(sc + 1) * 128], identf[:, :],
                           [finB[c], constB], [bkB])
                    if g == 0:
                        p.op("act", lambda e, bk=bk, ib=ib: e.activation(out=io[ib][:, 0:512], in_=bk[:, :], func=AF.Copy),
                             reads=[bkB], writes=[ioB[ib]])
                    else:
                        p.op("dve", lambda e, bk=bk, ib=ib: e.tensor_copy(out=io[ib][:, 512:1024], in_=bk[:, :]),
                             reads=[bkB], writes=[ioB[ib]])
                ev = p.dma("act", out[s, t0 + sc * 128: t0 + (sc + 1) * 128, :], io[ib][:, :], reads=[ioB[ib]])
                out_events.append(ev)

        tiles = [(s, t) for s in range(NSEQ) for t in range(NT)]
        xin_load(*tiles[0])
        xin_tr(*tiles[0])
        wst["inline"] = True
        if stop >= 1:
            layer0_a(*tiles[0])
        for i, (s, t) in enumerate(tiles):
            nxt = tiles[i + 1] if i + 1 < len(tiles) else None
            wst["inline"] = (i == 0)
            if nxt is not None and i > 0:
                xin_load(*nxt)
            if stop >= 1:
                layer0_b(s, t)
            if stop >= 2:
                p.tag = "MLP0"
                mlp(0, 1)
            if stop >= 3:
                layer1(s, t)
            if stop >= 4:
                p.tag = "MLP1"
                mlp(1, 3)
            if nxt is not None and i == 0:
                xin_load(*nxt)
            fin_norm()
            wst["inline"] = False
            if nxt is not None:
                xin_tr(*nxt)
                if stop >= 1:
                    layer0_a(*nxt)
            fin_out(s, t)
        p.emit(final_events=out_events)
        nc._ptags = p.tags
    return nc


def host_consts():
    half = 128
    inv = np.exp(-np.log(10000.0) * np.arange(half, dtype=np.float32) / half).astype(np.float32)
    pos = np.arange(SEQ, dtype=np.float32)
    ang = (pos[:, None] * inv[None, :]).astype(np.float32)
    cosT = np.ascontiguousarray(np.cos(ang.astype(np.float64)).T).astype(np.float32)
    sinT = np.ascontiguousarray(np.sin(ang.astype(np.float64)).T).astype(np.float32)
    lg = np.log1p(-np.exp2(-5.0 - np.arange(4, dtype=np.float64)))
    idx = np.arange(128, dtype=np.float64)
    i = idx[None, :]
    j = idx[:, None]
    allowed = (j <= i) | ((j // 64) == (i // 64))
    maskT = np.zeros((128, 4, 128), np.float32)
    qd = np.zeros((128, 4, 128), np.float32)
    kd = np.zeros((128, 4), np.float32)
    for hh in range(4):
        m = np.exp(lg[hh] * np.abs(i - j)) * allowed * (256 ** -0.5)
        maskT[:, hh, :] = m.astype(np.float32)
        qd[:, hh, :] = np.exp(lg[hh] * (idx + 1.0))[None, :].astype(np.float32)
        kd[:, hh] = (np.exp(lg[hh] * (127.0 - idx)) * (256 ** -0.5)).astype(np.float32)
    identf = np.eye(128, dtype=np.float32)
    return dict(cosT=cosT, sinT=sinT, maskT=maskT, qd=qd, kd=kd, identf=identf)


def bias_table(rel_bias):
    j = np.arange(128)[:, None, None]
    d = np.arange(5)[None, :, None]
    i = np.arange(128)[None, None, :]
    rel = j - i - 128 * d
    bidx = np.maximum(rel, -256) + 256
    valid = (rel <= 63 - (i % 64)) & (rel >= -(i % 64) - 512)
    bidx = np.clip(bidx, 0, 319)
    tab = rel_bias[:, bidx]
    tab = np.where(valid[None], tab, np.float32(-30000.0)).astype(np.float32)
    return np.ascontiguousarray(tab.reshape(16, 128, 640))


def make_in_maps(inputs, ncores, nseq, S):
    hc = host_consts()
    g = np.stack([inputs["mix_norm_g"][0], inputs["mlp_norm_g"][0], inputs["mix_norm_g"][1],
                  inputs["mlp_norm_g"][1], inputs["final_norm_g"]], 0)
    gains = np.ascontiguousarray(g.reshape(5, 8, 128).transpose(2, 0, 1)).astype(np.float32)
    common = dict(
        win0=np.ascontiguousarray(inputs["ret_w_in"][0]), wout0=np.ascontiguousarray(inputs["ret_w_out"][0]),
        w1_0=np.ascontiguousarray(inputs["mlp_w1"][0]), w2_0=np.ascontiguousarray(inputs["mlp_w2"][0]),
        win1=np.ascontiguousarray(inputs["att_w_in"][0]), wout1=np.ascontiguousarray(inputs["att_w_out"][0]),
        w1_1=np.ascontiguousarray(inputs["mlp_w1"][1]), w2_1=np.ascontiguousarray(inputs["mlp_w2"][1]),
        gains=gains, gng=np.ascontiguousarray(inputs["ret_gn_g"][0][None, :]),
        biasT=bias_table(np.asarray(inputs["att_rel_bias"][0], np.float32)), **hc)
    x = np.asarray(inputs["x"], np.float32)
    maps = []
    for c in range(ncores):
        m = dict(common)
        m["x"] = np.ascontiguousarray(x[c * nseq:(c + 1) * nseq, :S, :])
        maps.append(m)
    return maps


_NC_CACHE = {}


def kernel(**inputs):
    inputs = {k: np.asarray(v) for k, v in inputs.items()}
    NSC = 2
    NT = SEQ // (NSC * 128)
    nseq = 16 // NCORES
    key = (nseq, NT, NSC)
    if key not in _NC_CACHE:
        _NC_CACHE[key] = build(nseq, NT, NSC)
    nc = _NC_CACHE[key]
    maps = make_in_maps(inputs, NCORES, nseq, SEQ)
    res = run_bass_kernel_spmd(nc, maps, core_ids=list(range(NCORES)))
    outs = [np.asarray(r["out"]) for r in res.results]
    return np.concatenate(outs, axis=0).astype(np.float32)
```

```python
import numpy as np
from contextlib import ExitStack
import concourse.bass as bass
import concourse.mybir as mybir
from concourse.bass_utils import run_bass_kernel_spmd

F32 = mybir.dt.float32
BF16 = mybir.dt.bfloat16
AF = mybir.ActivationFunctionType
ALU = mybir.AluOpType

D = 1024
SEQ = 4096
NCORES = 8
EPS = 1e-6
NDS = 32


class Buf:
    __slots__ = ("name", "w", "r", "peers")

    def __init__(self, name=""):
        self.name = name
        self.w = None
        self.r = []
        self.peers = []


class Prog:
    ENG = ("pe", "act", "dve", "pool", "sp")

    def __init__(self, nc, es):
        self.nc = nc
        self.ops = {e: [] for e in self.ENG}
        self.sem = {e: es.enter_context(nc.semaphore("s_" + e)) for e in self.ENG}
        self.dsem = [es.enter_context(nc.semaphore(f"dq{i}")) for i in range(NDS)]
        self.dcnt = [0] * NDS
        self.dnext = 0
        self.pool_hist = []
        self.tag = ""
        self.tags = {e: [] for e in self.ENG}

    def _deps(self, eng, reads, writes):
        deps = []
        for b in reads:
            if b.w is not None:
                deps.append(b.w)
        for b in writes:
            if b.w is not None:
                deps.append(b.w)
            deps.extend(b.r)
            for pb in b.peers:
                if pb.w is not None:
                    deps.append(pb.w)
                deps.extend(pb.r)
        out = []
        seen = set()
        for d in deps:
            if d[0] == "e" and d[1] == eng == "pe":
                continue
            if d not in seen:
                seen.add(d)
                out.append(d)
        return out

    def _commit(self, ev, reads, writes):
        for b in reads:
            b.r.append(ev)
        for b in writes:
            b.w = ev
            b.r = []

    def op(self, eng, fn, reads=(), writes=()):
        deps = self._deps(eng, reads, writes)
        for d in deps:
            if d[0] == "e":
                self.ops[d[1]][d[2]][4] = True
        idx = len(self.ops[eng])
        self.tags[eng].append(self.tag)
        self.ops[eng].append([fn, deps, "op", None, False])
        self._commit(("e", eng, idx), reads, writes)

    def dma(self, eng, out, in_, reads=(), writes=(), **kw):
        deps = self._deps(eng, reads, writes)
        for d in deps:
            if d[0] == "e":
                self.ops[d[1]][d[2]][4] = True
        k = self.dnext
        self.dnext = (self.dnext + 1) % NDS
        if self.dcnt[k] > 0:
            deps.append(("d", k, self.dcnt[k]))
        if eng == "pool" and len(self.pool_hist) >= 2:
            deps.append(self.pool_hist[-2])
        self.dcnt[k] += 16
        ev = ("d", k, self.dcnt[k])
        if eng == "pool":
            self.pool_hist.append(ev)

        def fn(e, out=out, in_=in_, kw=kw):
            return e.dma_start(out=out, in_=in_, **kw)
        self.tags[eng].append(self.tag)
        self.ops[eng].append([fn, deps, "dma", k, False])
        self._commit(ev, reads, writes)
        return ev

    def emit(self, final_events=()):
        nc = self.nc
        pref = {}
        for e in self.ENG:
            c = 0
            arr = []
            for o in self.ops[e]:
                if o[2] == "op" and o[4]:
                    c += 1
                arr.append(c)
            pref[e] = arr

        def run(e, eo):
            seen = {}
            for (fn, deps, kind, k, needs) in self.ops[e]:
                for d in deps:
                    if d[0] == "e":
                        s = self.sem[d[1]]
                        v = pref[d[1]][d[2]]
                    else:
                        s = self.dsem[d[1]]
                        v = d[2]
                    if seen.get(s.num, 0) >= v:
                        continue
                    seen[s.num] = v
                    eo.wait_ge(s, v)
                ins = fn(eo)
                if kind == "dma":
                    ins.then_inc(self.dsem[k], 16)
                elif needs:
                    ins.then_inc(self.sem[e], 1)
            if e == "sp":
                for d in final_events:
                    if d[0] == "d":
                        eo.wait_ge(self.dsem[d[1]], d[2])
                    else:
                        eo.wait_ge(self.sem[d[1]], pref[d[1]][d[2]])

        with nc.Block() as block:
            @block.tensor
            def _(eo):
                run("pe", eo)

            @block.scalar
            def _(eo):
                run("act", eo)

            @block.vector
            def _(eo):
                run("dve", eo)

            @block.gpsimd
            def _(eo):
                run("pool", eo)

            @block.sync
            def _(eo):
                run("sp", eo)


WSPEC = [
    ("win0", 1024, 6144), ("wout0", 2048, 1024), ("w1_0", 1024, 4096), ("w2_0", 4096, 1024),
    ("win1", 1024, 3072), ("wout1", 1024, 1024), ("w1_1", 1024, 4096), ("w2_1", 4096, 1024),
]
WINFO = {}
_off = 0
for _n, _K, _N in WSPEC:
    _KC = _K // 128
    _NW = 4096 // _KC
    WINFO[_n] = (_off, _KC, _NW, _N // _NW)
    _off += _N // _NW
NPIECES = _off


def build(NSEQ, NT, NSC, final_norm=True, stop=99):
    T = NSC * 128
    S = NT * T
    NB = 4 + NSC
    nc = bass.Bass("TRN2", target_bir_lowering=False)

    def din(name, shape, dt=F32):
        return nc.dram_tensor(name, list(shape), dt, kind="ExternalInput").ap()

    x = din("x", [NSEQ, S, D])
    wd = {n: din(n, [K, N]) for n, K, N in WSPEC}
    gains = din("gains", [128, 5, 8])
    gng = din("gng", [1, 2048])
    biasT = din("biasT", [16, 128, 640])
    identf_d = din("identf", [128, 128])
    cosT = din("cosT", [128, SEQ])
    sinT = din("sinT", [128, SEQ])
    maskT_d = din("maskT", [128, 4, 128])
    qd_d = din("qd", [128, 4, 128])
    kd_d = din("kd", [128, 4])
    out = nc.dram_tensor("out", [NSEQ, S, D], F32, kind="ExternalOutput").ap()
    wsc = nc.dram_tensor("wsc", [NPIECES, 128, 4096], BF16, kind="Internal").ap()
    bsc = nc.dram_tensor("bsc", [16, 128, 640], BF16, kind="Internal").ap()

    lg = [np.log1p(-2.0 ** (-5.0 - h)) for h in range(4)]
    cdec = [float(np.exp(lg[h] * 128.0)) for h in range(4)]

    with ExitStack() as es:
        p = Prog(nc, es)

        def sb(name, shape, dt):
            return es.enter_context(nc.sbuf_tensor(name, list(shape), dt))

        h = sb("h", [128, 8, T], F32)
        hB = [Buf(f"h{c}") for c in range(8)]
        hn = sb("hn", [128, 8, T], BF16)
        hnB = [Buf(f"hn{c}") for c in range(8)]
        sq = sb("sq", [128, 8, T], BF16)
        sqB = Buf("sq")
        NSLOT = 4
        wslot = [sb(f"ws{i}", [128, 4096], BF16) for i in range(NSLOT)]
        wslotB = [Buf(f"ws{i}") for i in range(NSLOT)]
        state = sb("state", [128, 8, 512], F32)
        stateB = [Buf(f"st{i}") for i in range(8)]
        state_bf = sb("state_bf", [128, 8, 512], BF16)
        statebB = [Buf(f"stb{i}") for i in range(8)]
        kwin = sb("kwin", [128, 8, NB * 128], BF16)
        kwinB = [[Buf(f"kw{c}_{r}") for r in range(NB)] for c in range(8)]
        vwin = sb("vwin", [128, NB, 1024], BF16)
        vwinB = [[Buf(f"vw{r}_{j}") for j in range(2)] for r in range(NB)]
        gains_sb = sb("gains_sb", [128, 5, 8], F32)
        gnb = sb("gnb", [128, 2048], F32)
        identf = sb("identf_sb", [128, 128], F32)
        identb = sb("identb", [128, 128], BF16)
        ones_bf = sb("ones_bf", [128, 128], BF16)
        maskT = sb("maskT_sb", [128, 4, 128], F32)
        qd = sb("qd_sb", [128, 4, 128], F32)
        kd = sb("kd_sb", [128, 4], F32)
        epst = sb("epst", [128, 1], F32)
        constB = Buf("const")
        cos_sb = [sb(f"cos{i}", [128, T], F32) for i in range(2)]
        sin_sb = [sb(f"sin{i}", [128, T], F32) for i in range(2)]
        ropeB = [Buf(f"rope{i}") for i in range(2)]
        bias_sb = [sb(f"bias{i}", [128, 640], BF16) for i in range(4)]
        biasB = [Buf(f"bias{i}") for i in range(4)]
        io = [sb(f"io{i}", [128, 1024], F32) for i in range(2)]
        ioB = [Buf(f"io{i}") for i in range(2)]
        rstd = sb("rstd", [128, T], F32)
        rstdB = Buf("rstd")
        small = sb("small", [128, 128], F32)
        smallB = [Buf("small0"), Buf("small1")]

        ARENA = 65 * 1024
        arena = sb("arena", [128, ARENA // 2], BF16)
        carved = []

        def carve(phase, off, shape, dt, nbufs=1, name=""):
            esz = 4 if dt == F32 else 2
            n = int(np.prod(shape))
            nbytes = n * esz
            assert off % 4 == 0 and off + nbytes <= ARENA, (name, off, nbytes)
            ap = arena[:, off // 2: (off + nbytes) // 2]
            if dt == F32:
                ap = ap.bitcast(F32)
            if len(shape) == 2:
                ap = ap.rearrange("p (a b) -> p a b", a=shape[0])
            elif len(shape) == 3:
                ap = ap.rearrange("p (a b c) -> p a b c", a=shape[0], b=shape[1])
            bufs = [Buf(f"{name}{i}") for i in range(nbufs)]
            for (ph, s0, e0, bs) in carved:
                if ph != phase and s0 < off + nbytes and off < e0:
                    for b in bufs:
                        b.peers.extend(bs)
                    for b in bs:
                        b.peers.extend(bufs)
            carved.append((phase, off, off + nbytes, bufs))
            return ap, bufs, off + nbytes

        o_ = 0
        qT, qTB, o_ = carve("L0", o_, [8, T], BF16, 8, "qT")
        kT, kTB, o_ = carve("L0", o_, [8, T], BF16, 8, "kT")
        v0, v0B, o_ = carve("L0", o_, [NSC, 2048], BF16, NSC * 4, "v0")
        sg, sgB, o_ = carve("L0", o_, [NSC, 2048], BF16, NSC * 4, "sg")
        yT, yTB, o_ = carve("L0", o_, [16, T], BF16, NSC * 2, "yT")
        qdec, qdecB, o_ = carve("L0", o_, [2, 8, 128], BF16, 2, "qdec")
        kdec, kdecB, o_ = carve("L0", o_, [2, 1024], BF16, 2, "kdec")
        scb, scbB, o_ = carve("L0", o_, [2, 512], BF16, 2, "scb")
        ytm, ytmB, o_ = carve("L0", o_, [2, 2048], BF16, 8, "ytm")
        rtmp, rtmpB, o_ = carve("L0", o_, [4, T], F32, 4, "rtmp")
        onb, onbB, o_ = carve("L0", o_, [2, 512], F32, 2, "onb")
        gtmp, gtmpB, o_ = carve("L0", o_, [2, 512], F32, 2, "gtmp")
        crs, crsB, o_ = carve("L0", o_, [2, T], F32, 2, "crs")
        L0_END = o_
        o_ = 0
        uT, uTB, o_ = carve("MLP", o_, [32, T], BF16, 32, "uT")
        rl, rlB, o_ = carve("MLP", o_, [3, T], F32, 3, "rl")
        o_ = 0
        qm, qmB, o_ = carve("L1", o_, [16, T], BF16, 16, "qm")
        oT1, oT1B, o_ = carve("L1", o_, [8, T], BF16, 8, "oT1")
        PT, PTB, o_ = carve("L1", o_, [2, NB, T], BF16, 2 * NB, "PT")
        rec, recB, o_ = carve("L1", o_, [2, T], F32, 2, "rec")
        o_ = 8192
        fin, finB, o_ = carve("FIN", o_, [8, T], F32, 8, "fin")

        banks = [es.enter_context(nc.psum_tensor(f"pb{i}", [128, 512], F32)) for i in range(8)]
        bankB = [Buf(f"pb{i}") for i in range(8)]
        bstate = {"i": 0}

        bankH = [[Buf(f"pb{i}h{j}") for j in range(2)] for i in range(8)]
        for i in range(8):
            for hb_ in bankH[i]:
                hb_.peers.append(bankB[i])
                bankB[i].peers.append(hb_)

        held = set()

        def bank(hold=False):
            i = bstate["i"]
            while i in held:
                i = (i + 1) % 8
            bstate["i"] = (i + 1) % 8
            if hold:
                held.add(i)
            return banks[i], bankB[i]

        def release(bkB):
            held.discard(bankB.index(bkB))

        def bank_h():
            i = bstate["i"]
            bstate["i"] = (i + 1) % 8
            return banks[i], bankH[i]

        pend = {"fn": None, "cnt": 0}

        def mm(outap, lhsT, rhs, start, stop, reads, writes):
            p.op("pe", lambda e, o=outap, l=lhsT, r=rhs, s=start, t=stop:
                 e.matmul(o, lhsT=l, rhs=r, start=s, stop=t), reads=reads, writes=writes)
            if pend["fn"] is not None:
                pend["cnt"] += 1
                if pend["cnt"] >= 8:
                    f = pend["fn"]
                    pend["fn"] = None
                    f()

        def tr(outap, inap, ident, reads, writes):
            p.op("pe", lambda e, o=outap, i=inap, d=ident: e.transpose(o, i, d), reads=reads, writes=writes)

        wpieceB = [Buf(f"wp{i}") for i in range(NPIECES)]
        wst = {"slot": 0}

        wst["inline"] = False
        wst["q"] = 0

        def wload(name, j):
            off, KC, NW, npc = WINFO[name]
            pi = off + j
            s = wst["slot"]
            wst["slot"] = (s + 1) % NSLOT
            if not wst["inline"]:
                p.dma("sp", wslot[s][:, :], wsc[pi, :, :], reads=[wpieceB[pi]], writes=[wslotB[s]])
            else:
                src = wd[name].rearrange("(kc p) n -> p kc n", p=128)
                kq = KC // 4
                for qq in range(4):
                    qi = wst["q"] % 4
                    wst["q"] += 1
                    stg, stgB = stage4[qi]
                    p.dma("sp", stg[:, :].rearrange("p (k n) -> p k n", k=kq),
                          src[:, qq * kq:(qq + 1) * kq, j * NW:(j + 1) * NW], writes=[stgB])
                    dst = wslot[s][:, qq * 1024:(qq + 1) * 1024]
                    if qq % 2 == 0:
                        p.op("act", lambda e, dst=dst, stg=stg: e.activation(out=dst, in_=stg[:, :], func=AF.Copy),
                             reads=[stgB], writes=[wslotB[s]])
                    else:
                        p.op("dve", lambda e, dst=dst, stg=stg: e.tensor_copy(out=dst, in_=stg[:, :]),
                             reads=[stgB], writes=[wslotB[s]])
                p.dma("act", wsc[pi, :, :], wslot[s][:, :], reads=[wslotB[s]], writes=[wpieceB[pi]])
            return wslot[s][:, :].rearrange("p (k n) -> p k n", k=KC), wslotB[s]

        p.dma("sp", gains_sb[:], gains[:, :, :], writes=[constB])
        p.dma("sp", gnb[:], gng.partition_broadcast(128) if False else gng[0:1, :].broadcast_to([128, 2048]), writes=[constB])
        p.dma("sp", identf[:], identf_d[:, :], writes=[constB])
        p.dma("sp", maskT[:], maskT_d[:, :, :], writes=[constB])
        p.dma("sp", qd[:], qd_d[:, :, :], writes=[constB])
        p.dma("sp", kd[:], kd_d[:, :], writes=[constB])
        p.op("dve", lambda e: e.tensor_copy(out=identb[:], in_=identf[:]), reads=[constB], writes=[constB])
        p.op("dve", lambda e: e.memset(ones_bf[:], 1.0), writes=[constB])
        p.op("dve", lambda e: e.memset(epst[:], EPS), writes=[constB])
        stg32, stg32B, o2_ = carve("PRO", 0, [2, 4096], F32, 2, "stg32")
        stgbf, stgbfB, o2_ = carve("PRO", o2_, [2, 4096], BF16, 2, "stgbf")
        pi_ = 0
        bscB = Buf("bsc")
        for hd in range(16):
            b = pi_ % 2
            p.dma("sp", stg32[:, b, 0:640], biasT[hd, :, :], writes=[stg32B[b]])
            p.op("dve", lambda e, b=b: e.tensor_copy(out=stgbf[:, b, 0:640], in_=stg32[:, b, 0:640]),
                 reads=[stg32B[b]], writes=[stgbfB[b]])
            p.dma("act", bsc[hd, :, :], stgbf[:, b, 0:640], reads=[stgbfB[b]], writes=[bscB])
            pi_ += 1

        rcol = sb("rcol", [128, 4], F32)
        rcolB = Buf("rcol")
        rs2 = sb("rs2", [128, T], F32)
        rs2B = Buf("rs2")

        def rmsnorm(gi, out_fin=False, need_col=False, need_sq=False, after=None):
            for c in range(8):
                dst = fin if out_fin else hn
                dstB = finB if out_fin else hnB
                if c % 2 == 0:
                    p.op("act", lambda e, c=c, dst=dst: e.activation(out=dst[:, c, :], in_=h[:, c, :], func=AF.Identity,
                                                                     scale=gains_sb[:, gi, c:c + 1]),
                         reads=[hB[c], constB], writes=[dstB[c]])
                else:
                    p.op("dve", lambda e, c=c, dst=dst: e.tensor_scalar(out=dst[:, c, :], in0=h[:, c, :],
                                                                        scalar1=gains_sb[:, gi, c:c + 1], scalar2=None,
                                                                        op0=ALU.mult),
                         reads=[hB[c], constB], writes=[dstB[c]])
            p.op("act", lambda e: e.activation(out=sq[:], in_=h[:], func=AF.Square),
                 reads=hB, writes=[sqB])

            def stats():
                bk, bkB = bank()
                for c in range(8):
                    mm(bk[:, 0:T], ones_bf[:, :], sq[:, c, :], c == 0, c == 7, [sqB, constB], [bkB])
                if need_col:
                    bc, bcB = bank()
                    for sc in range(NSC):
                        for c in range(8):
                            mm(bc[:, sc:sc + 1], sq[:, c, sc * 128:(sc + 1) * 128], ones_bf[:, 0:1], c == 0, c == 7,
                               [sqB, constB], [bcB])
                p.op("act", lambda e, bk=bk: e.activation(out=rstd[:], in_=bk[:, 0:T], func=AF.Sqrt,
                                                          scale=1.0 / D, bias=epst[:, 0:1]),
                     reads=[bkB, constB], writes=[rstdB])
                p.op("dve", lambda e: e.reciprocal(out=rstd[:], in_=rstd[:]), reads=[rstdB], writes=[rstdB])
                if need_col:
                    p.op("act", lambda e, bc=bc: e.activation(out=rcol[:, 0:NSC], in_=bc[:, 0:NSC], func=AF.Sqrt,
                                                              scale=1.0 / D, bias=epst[:, 0:1]),
                         reads=[bcB, constB], writes=[rcolB])
                    p.op("dve", lambda e: e.reciprocal(out=rcol[:, 0:NSC], in_=rcol[:, 0:NSC]), reads=[rcolB], writes=[rcolB])
                if need_sq:
                    p.op("dve", lambda e: e.tensor_tensor(out=rs2[:], in0=rstd[:], in1=rstd[:], op=ALU.mult),
                         reads=[rstdB], writes=[rs2B])
                if out_fin:
                    for c in range(8):
                        p.op("dve", lambda e, c=c: e.tensor_tensor(
                            out=fin[:, c, :], in0=fin[:, c, :], in1=rstd[:], op=ALU.mult),
                            reads=[rstdB, finB[c]], writes=[finB[c]])
                if after is not None:
                    after()

            if out_fin:
                stats()
            else:
                assert pend["fn"] is None
                pend["fn"] = stats
                pend["cnt"] = 0

        def mlp(layer, gi):
            rmsnorm(gi, need_sq=True)
            n1, n2 = f"w1_{layer}", f"w2_{layer}"
            for j in range(8):
                wp, wB = wload(n1, j)
                for q in range(4):
                    fc = j * 4 + q
                    bk, bkB = bank()
                    for kc in range(8):
                        mm(bk[:, 0:T], wp[:, kc, q * 128:(q + 1) * 128], hn[:, kc, :], kc == 0, kc == 7,
                           [wB, hnB[kc]], [bkB])
                    r = fc % 3
                    p.op("act", lambda e, bk=bk, r=r: e.activation(out=rl[:, r, :], in_=bk[:, 0:T], func=AF.Relu),
                         reads=[bkB], writes=[rlB[r]])
                    p.op("dve", lambda e, r=r, fc=fc: e.tensor_tensor(out=uT[:, fc, :], in0=rl[:, r, :],
                                                                      in1=rl[:, r, :], op=ALU.mult),
                         reads=[rlB[r]], writes=[uTB[fc]])
            for j in range(8):
                wp, wB = wload(n2, j)
                bk, bkB = bank()
                for fc in range(32):
                    mm(bk[:, 0:T], wp[:, fc, :], uT[:, fc, :], fc == 0, fc == 31, [wB, uTB[fc]], [bkB])
                r = j % 3
                p.op("dve", lambda e, bk=bk, r=r: e.tensor_tensor(out=rl[:, r, :], in0=bk[:, 0:T], in1=rs2[:],
                                                                  op=ALU.mult), reads=[bkB, rs2B], writes=[rlB[r]])
                p.op("dve", lambda e, j=j, r=r: e.tensor_tensor(out=h[:, j, :], in0=h[:, j, :], in1=rl[:, r, :],
                                                                op=ALU.add), reads=[rlB[r], hB[j]], writes=[hB[j]])

        out_events = []
        xin = [sb(f"xin{i}", [128, 1024], F32) for i in range(2)]
        xinB = [Buf(f"xin{i}") for i in range(2)]
        NBIAS = 4
        stage4 = [(io[0], ioB[0]), (io[1], ioB[1]), (xin[0], xinB[0]), (xin[1], xinB[1])]

        def xin_load(s, t):
            t0 = t * T
            for sc in range(NSC):
                p.dma("sp", xin[sc % 2][:, :], x[s, t0 + sc * 128: t0 + (sc + 1) * 128, :], writes=[xinB[sc % 2]])

        def xin_tr(s, t):
            p.tag = "xin"
            for sc in range(NSC):
                ib = sc % 2
                for g in range(2):
                    bk, bkB = bank()
                    for q in range(4):
                        c = g * 4 + q
                        tr(bk[:, q * 128:(q + 1) * 128], xin[ib][:, c * 128:(c + 1) * 128], identf[:, :],
                           [xinB[ib], constB], [bkB])
                    if g == 0:
                        p.op("act", lambda e, bk=bk, g=g, sc=sc: e.activation(
                            out=h[:, g * 4:(g + 1) * 4, sc * 128:(sc + 1) * 128],
                            in_=bk[:, :].rearrange("p (a b) -> p a b", a=4), func=AF.Copy),
                            reads=[bkB], writes=hB[g * 4:(g + 1) * 4])
                    else:
                        p.op("dve", lambda e, bk=bk, g=g, sc=sc: e.tensor_copy(
                            out=h[:, g * 4:(g + 1) * 4, sc * 128:(sc + 1) * 128],
                            in_=bk[:, :].rearrange("p (a b) -> p a b", a=4)),
                            reads=[bkB], writes=hB[g * 4:(g + 1) * 4])

        def layer0_a(s, t):
            t0 = t * T
            par = t % 2
            p.dma("sp", cos_sb[par][:, :], cosT[:, t0:t0 + T], writes=[ropeB[par]])
            p.dma("sp", sin_sb[par][:, :], sinT[:, t0:t0 + T], writes=[ropeB[par]])
            p.tag = "L0.norm"
            def scaled_tables():
                p.op("dve", lambda e: e.tensor_tensor(out=crs[:, 0, :], in0=cos_sb[par][:, :], in1=rstd[:], op=ALU.mult),
                     reads=[ropeB[par], rstdB], writes=[crsB[0]])
                p.op("dve", lambda e: e.tensor_tensor(out=crs[:, 1, :], in0=sin_sb[par][:, :], in1=rstd[:], op=ALU.mult),
                     reads=[ropeB[par], rstdB], writes=[crsB[1]])

            rmsnorm(0, need_col=True, after=scaled_tables)
            p.tag = "L0.qk"
            for pj in range(4):
                wp, wB = wload("win0", pj)
                for hp in range(2):
                    cpair = pj * 4 + hp * 2
                    bks = []
                    for dc in range(2):
                        bk, bkB = bank()
                        q = hp * 2 + dc
                        for kc in range(8):
                            mm(bk[:, 0:T], wp[:, kc, q * 128:(q + 1) * 128], hn[:, kc, :], kc == 0, kc == 7,
                               [wB, hnB[kc]], [bkB])
                        bks.append((bk, bkB))
                    (b1, b1B), (b2, b2B) = bks
                    isq = cpair < 8
                    dst = qT if isq else kT
                    dstB = qTB if isq else kTB
                    c1 = cpair % 8
                    cs, sn = crs[:, 0, :], crs[:, 1, :]
                    p.op("dve", lambda e, b1=b1, cs=cs: e.tensor_tensor(out=rtmp[:, 0, :], in0=b1[:, 0:T], in1=cs, op=ALU.mult),
                         reads=[b1B, crsB[0]], writes=[rtmpB[0]])
                    p.op("dve", lambda e, b2=b2, sn=sn: e.tensor_tensor(out=rtmp[:, 1, :], in0=b2[:, 0:T], in1=sn, op=ALU.mult),
                         reads=[b2B, crsB[1]], writes=[rtmpB[1]])
                    p.op("dve", lambda e, b1=b1, sn=sn: e.tensor_tensor(out=rtmp[:, 2, :], in0=b1[:, 0:T], in1=sn, op=ALU.mult),
                         reads=[b1B, crsB[1]], writes=[rtmpB[2]])
                    p.op("dve", lambda e, b2=b2, cs=cs: e.tensor_tensor(out=rtmp[:, 3, :], in0=b2[:, 0:T], in1=cs, op=ALU.mult),
                         reads=[b2B, crsB[0]], writes=[rtmpB[3]])
                    p.op("pool", lambda e, dst=dst, c1=c1: e.tensor_tensor(out=dst[:, c1, :], in0=rtmp[:, 0, :], in1=rtmp[:, 1, :], op=ALU.subtract),
                         reads=[rtmpB[0], rtmpB[1]], writes=[dstB[c1]])
                    p.op("pool", lambda e, dst=dst, c1=c1: e.tensor_tensor(out=dst[:, c1 + 1, :], in0=rtmp[:, 2, :], in1=rtmp[:, 3, :], op=ALU.add),
                         reads=[rtmpB[2], rtmpB[3]], writes=[dstB[c1 + 1]])

        def layer0_b(s, t):
            p.tag = "L0.vg"

            def vg_piece(pj):
                wp, wB = wload("win0", 4 + pj)
                for sc in range(NSC):
                    bk, bkB = bank()
                    for kc in range(8):
                        mm(bk[:, :], hn[:, kc, sc * 128:(sc + 1) * 128], wp[:, kc, :], kc == 0, kc == 7,
                           [wB, hnB[kc]], [bkB])
                    if pj < 4:
                        p.op("act", lambda e, bk=bk, sc=sc, pj=pj: e.activation(
                            out=v0[:, sc, pj * 512:(pj + 1) * 512], in_=bk[:, :], func=AF.Identity,
                            scale=rcol[:, sc:sc + 1]),
                            reads=[bkB, rcolB], writes=[v0B[sc * 4 + pj]])
                    else:
                        g4 = pj - 4
                        gi_ = (sc + pj) % 2
                        p.op("act", lambda e, bk=bk, gi_=gi_, sc=sc: e.activation(
                            out=gtmp[:, gi_, :], in_=bk[:, :], func=AF.Silu, scale=rcol[:, sc:sc + 1]),
                            reads=[bkB, rcolB], writes=[gtmpB[gi_]])
                        p.op("dve", lambda e, sc=sc, g4=g4, gi_=gi_: e.tensor_tensor(
                            out=sg[:, sc, g4 * 512:(g4 + 1) * 512], in0=gtmp[:, gi_, :],
                            in1=gnb[:, g4 * 512:(g4 + 1) * 512], op=ALU.mult),
                            reads=[gtmpB[gi_], constB], writes=[sgB[sc * 4 + g4]])

            for pj in range(4):
                vg_piece(pj)
            p.tag = "L0.ret"

            def ytr(sc):
                pb = sc % 2
                cols = slice(sc * 128, (sc + 1) * 128)
                for g in range(2):
                    bk, bkB = bank()
                    bkb = bk[:, :].bitcast(BF16)
                    for q in range(8):
                        ec = g * 8 + q
                        tr(bkb[:, q * 128:(q + 1) * 128], ytm[:, pb, ec * 128:(ec + 1) * 128], identb[:, :],
                           [ytmB[pb * 4 + ec // 4], constB], [bkB])
                    p.op("act", lambda e, bkb=bkb, g=g, cols=cols: e.activation(
                        out=yT[:, g * 8:(g + 1) * 8, cols], in_=bkb[:, :].rearrange("p (a b) -> p a b", a=8),
                        func=AF.Copy), reads=[bkB], writes=[yTB[sc * 2 + g]])

            for sc in range(NSC):
                first = (t == 0 and sc == 0)
                last = (t == NT - 1 and sc == NSC - 1)
                pb = sc % 2
                cols = slice(sc * 128, (sc + 1) * 128)
                for c in range(8):
                    p.op("pool", lambda e, c=c, pb=pb, cols=cols: e.tensor_tensor(
                        out=qdec[:, pb, c, :], in0=qT[:, c, cols], in1=qd[:, c // 2, :], op=ALU.mult),
                        reads=[qTB[c], constB], writes=[qdecB[pb]])
                bk, bkB = bank()
                bkb = bk[:, :].bitcast(BF16)
                for c in range(8):
                    tr(bkb[:, c * 128:(c + 1) * 128], kT[:, c, cols], identb[:, :], [kTB[c], constB], [bkB])
                for hh in range(4):
                    p.op("act", lambda e, bkb=bkb, hh=hh, pb=pb: e.activation(
                        out=kdec[:, pb, hh * 256:(hh + 1) * 256], in_=bkb[:, hh * 256:(hh + 1) * 256],
                        func=AF.Identity, scale=kd[:, hh:hh + 1]),
                        reads=[bkB, constB], writes=[kdecB[pb]])
                bs, bsB = bank()
                for hh in range(4):
                    for dc in range(2):
                        c = 2 * hh + dc
                        mm(bs[:, hh * 128:(hh + 1) * 128], kT[:, c, cols], qT[:, c, cols], dc == 0, dc == 1,
                           [kTB[c], qTB[c]], [bsB])
                p.op("dve", lambda e, bs=bs, pb=pb: e.tensor_tensor(
                    out=scb[:, pb, :], in0=bs[:, :], in1=maskT[:, :, :].rearrange("p a b -> p (a b)"), op=ALU.mult),
                    reads=[bsB, constB], writes=[scbB[pb]])
                obanks = []
                for hh in range(4):
                    bo, boB = bank(hold=True)
                    obanks.append((bo, boB))
                    mm(bo[:, :], scb[:, pb, hh * 128:(hh + 1) * 128], v0[:, sc, hh * 512:(hh + 1) * 512],
                       True, first, [scbB[pb], v0B[sc * 4 + hh]], [boB])
                    if not first:
                        for dc in range(2):
                            c = 2 * hh + dc
                            mm(bo[:, :], qdec[:, pb, c, :], state_bf[:, c, :], False, dc == 1,
                               [qdecB[pb], statebB[c]], [boB])
                if sc == 0:
                    p.tag = "L0.vg"
                    for pj in range(4, 8):
                        vg_piece(pj)
                    p.tag = "L0.ret"
                if sc > 0:
                    ytr(sc - 1)
                sm3 = small[:, pb * 64:(pb + 1) * 64].rearrange("p (a b) -> p a b", a=4)
                smB = smallB[pb]
                for hh in range(4):
                    bo, boB = obanks[hh]
                    p.op("dve", lambda e, bo=bo, hh=hh, sm3=sm3: e.bn_stats(out=sm3[:, hh, 0:6], in_=bo[:, :]),
                         reads=[boB], writes=[smB])
                for hh in range(4):
                    p.op("dve", lambda e, hh=hh, sm3=sm3: e.bn_aggr(out=sm3[:, hh, 6:8], in_=sm3[:, hh, 0:6]),
                         reads=[smB], writes=[smB])
                p.op("act", lambda e, sm3=sm3: e.activation(out=sm3[:, :, 8:9], in_=sm3[:, :, 7:8], func=AF.Sqrt,
                                                          bias=epst[:, 0:1], scale=1.0),
                     reads=[smB, constB], writes=[smB])

                def state_upd(clist):
                    for c in clist:
                        hh = c // 2
                        bu, buB = bank()
                        mm(bu[:, :], kdec[:, pb, c * 128:(c + 1) * 128], v0[:, sc, hh * 512:(hh + 1) * 512],
                           True, True, [kdecB[pb], v0B[sc * 4 + hh]], [buB])
                        if first:
                            p.op("dve", lambda e, bu=bu, c=c: e.tensor_copy(out=state[:, c, :], in_=bu[:, :]),
                                 reads=[buB], writes=[stateB[c]])
                        else:
                            p.op("dve", lambda e, bu=bu, c=c, hh=hh: e.scalar_tensor_tensor(
                                out=state[:, c, :], in0=state[:, c, :], scalar=cdec[hh], in1=bu[:, :],
                                op0=ALU.mult, op1=ALU.add), reads=[buB, stateB[c]], writes=[stateB[c]])
                        if sc == NSC - 1:
                            p.op("pool", lambda e, c=c: e.tensor_copy(out=state_bf[:, c, :], in_=state[:, c, :]),
                                 reads=[stateB[c]], writes=[statebB[c]])
                        else:
                            p.op("act", lambda e, c=c: e.activation(out=state_bf[:, c, :], in_=state[:, c, :], func=AF.Copy),
                                 reads=[stateB[c]], writes=[statebB[c]])

                if not last:
                    state_upd(range(0, 4))
                p.op("dve", lambda e, sm3=sm3: e.reciprocal(out=sm3[:, :, 9:10], in_=sm3[:, :, 8:9]),
                     reads=[smB], writes=[smB])
                p.op("dve", lambda e, sm3=sm3: e.scalar_tensor_tensor(
                    out=sm3[:, :, 10:11], in0=sm3[:, :, 6:7], scalar=-1.0, in1=sm3[:, :, 9:10], op0=ALU.mult, op1=ALU.mult),
                    reads=[smB], writes=[smB])
                for hh in range(4):
                    bo, boB = obanks[hh]
                    ob = hh % 2
                    p.op("act", lambda e, bo=bo, sm3=sm3, ob=ob, hh=hh: e.activation(
                        out=onb[:, ob, :], in_=bo[:, :], func=AF.Identity, scale=sm3[:, hh, 9:10], bias=sm3[:, hh, 10:11]),
                        reads=[boB, smB], writes=[onbB[ob]])
                    p.op("dve", lambda e, ob=ob, pb=pb, hh=hh, sc=sc: e.tensor_tensor(
                        out=ytm[:, pb, hh * 512:(hh + 1) * 512], in0=onb[:, ob, :],
                        in1=sg[:, sc, hh * 512:(hh + 1) * 512], op=ALU.mult),
                        reads=[onbB[ob], sgB[sc * 4 + hh]], writes=[ytmB[pb * 4 + hh]])
                    release(boB)
                if not last:
                    state_upd(range(4, 8))
            ytr(NSC - 1)
            p.tag = "L0.wout"
            for pj in range(4):
                wp, wB = wload("wout0", pj)
                for q in range(2):
                    ncn = pj * 2 + q
                    bk, bkB = bank()
                    for ec in range(16):
                        mm(bk[:, 0:T], wp[:, ec, q * 128:(q + 1) * 128], yT[:, ec, :], ec == 0, ec == 15,
                           [wB] + yTB, [bkB])
                    p.op("dve", lambda e, bk=bk, ncn=ncn: e.tensor_tensor(
                        out=h[:, ncn, :], in0=h[:, ncn, :], in1=bk[:, 0:T], op=ALU.add),
                        reads=[bkB, hB[ncn]], writes=[hB[ncn]])

        def layer1(s, t):
            p.tag = "L1.qkv"
            rmsnorm(2, need_col=True)
            A0 = t * NSC
            p.op("pool", lambda e: e.memset(qm[:], 0.0), writes=qmB)
            for pj in range(4):
                wp, wB = wload("win1", pj)
                for q in range(4):
                    c = (pj % 2) * 4 + q
                    bk, bkB = bank()
                    for kc in range(8):
                        mm(bk[:, 0:T], wp[:, kc, q * 128:(q + 1) * 128], hn[:, kc, :], kc == 0, kc == 7,
                           [wB, hnB[kc]], [bkB])
                    if pj < 2:
                        p.op("dve", lambda e, bk=bk, c=c: e.scalar_tensor_tensor(
                            out=qm[0:64, 2 * c, :], in0=bk[0:64, 0:T], scalar=0.125, in1=rstd[0:64, :],
                            op0=ALU.mult, op1=ALU.mult), reads=[bkB, rstdB], writes=[qmB[2 * c]])
                        p.op("dve", lambda e, bk=bk, c=c: e.scalar_tensor_tensor(
                            out=qm[64:128, 2 * c + 1, :], in0=bk[64:128, 0:T], scalar=0.125, in1=rstd[64:128, :],
                            op0=ALU.mult, op1=ALU.mult), reads=[bkB, rstdB], writes=[qmB[2 * c + 1]])
                    else:
                        for sc in range(NSC):
                            r = (A0 + sc) % NB
                            p.op("dve", lambda e, bk=bk, c=c, r=r, sc=sc: e.tensor_tensor(
                                out=kwin[:, c, r * 128:(r + 1) * 128], in0=bk[:, sc * 128:(sc + 1) * 128],
                                in1=rstd[:, sc * 128:(sc + 1) * 128], op=ALU.mult),
                                reads=[bkB, rstdB], writes=[kwinB[c][r]])
            for pj in range(2):
                wp, wB = wload("win1", 4 + pj)
                for sc in range(NSC):
                    r = (A0 + sc) % NB
                    bk, bkB = bank()
                    for kc in range(8):
                        mm(bk[:, :], hn[:, kc, sc * 128:(sc + 1) * 128], wp[:, kc, :], kc == 0, kc == 7,
                           [wB, hnB[kc]], [bkB])
                    p.op("act", lambda e, bk=bk, r=r, pj=pj, sc=sc: e.activation(
                        out=vwin[:, r, pj * 512:(pj + 1) * 512], in_=bk[:, :], func=AF.Identity,
                        scale=rcol[:, sc:sc + 1]),
                        reads=[bkB, rcolB], writes=[vwinB[r][pj]])
            KB0 = max(0, A0 - 4)
            KBs = list(range(KB0, A0 + NSC))
            p.tag = "L1.attn"
            wout1_pre = [wload("wout1", pj) for pj in range(2)]
            ent = []
            for KB in KBs:
                a_lo = max(KB, A0)
                a_hi = min(KB + 4, A0 + NSC - 1)
                if a_lo > a_hi:
                    continue
                ent.append((KB, (a_lo - A0) * 128, (a_hi - A0 + 1) * 128, a_lo - KB))

            def bias_load(hd):
                hb4 = hd % NBIAS
                p.dma("sp", bias_sb[hb4][:, :], bsc[hd, :, :], reads=[bscB], writes=[biasB[hb4]])

            def attn_a(hd):
                pc = hd // 2
                hb = hd % 2
                hb4 = hd % NBIAS
                if hd + NBIAS - 1 < 16:
                    bias_load(hd + NBIAS - 1)
                per_bank = 512 // T
                for g0 in range(0, len(ent), per_bank):
                    grp = ent[g0:g0 + per_bank]
                    bk, bkB = bank()
                    for gi_, (KB, qlo, qhi, d_lo) in enumerate(grp):
                        off = gi_ * T
                        r = KB % NB
                        mm(bk[:, off + qlo:off + qhi], kwin[:, pc, r * 128:(r + 1) * 128], qm[:, hd, qlo:qhi],
                           True, False, [kwinB[pc][r], qmB[hd]], [bkB])
                        mm(bk[:, off + qlo:off + qhi], identb[:, :],
                           bias_sb[hb4][:, d_lo * 128: d_lo * 128 + (qhi - qlo)], False, True, [constB, biasB[hb4]], [bkB])
                    for gi_, (KB, qlo, qhi, d_lo) in enumerate(grp):
                        off = gi_ * T
                        ki = KB - KB0
                        p.op("act", lambda e, bk=bk, off=off, qlo=qlo, qhi=qhi, hb=hb, ki=ki: e.activation(
                            out=PT[:, hb, ki, qlo:qhi], in_=bk[:, off + qlo:off + qhi], func=AF.Exp),
                            reads=[bkB], writes=[PTB[hb * NB + ki]])

            pair_banks = {}

            def attn_b(hd):
                pc, half = hd // 2, hd % 2
                prt = slice(half * 64, half * 64 + 64)
                hb = hd % 2
                if half == 0:
                    pair_banks[pc] = (bank(), bank())
                (bo, boB), (bsm, bsmB) = pair_banks[pc]
                for i, (KB, qlo, qhi, d_lo) in enumerate(ent):
                    r = KB % NB
                    ki = KB - KB0
                    p.op("pe", lambda e, bo=bo, prt=prt, qlo=qlo, qhi=qhi, r=r, hd=hd, hb=hb, ki=ki, i=i: e.matmul(
                        bo[prt, qlo:qhi], lhsT=vwin[:, r, hd * 64:(hd + 1) * 64], rhs=PT[:, hb, ki, qlo:qhi],
                        start=(i == 0), stop=(i == len(ent) - 1), skip_group_check=True),
                        reads=[vwinB[r][hd // 8], PTB[hb * NB + ki]], writes=[boB])
                for i, (KB, qlo, qhi, d_lo) in enumerate(ent):
                    ki = KB - KB0
                    p.op("pe", lambda e, bsm=bsm, prt=prt, qlo=qlo, qhi=qhi, hb=hb, ki=ki, i=i: e.matmul(
                        bsm[prt, qlo:qhi], lhsT=ones_bf[:, 0:64], rhs=PT[:, hb, ki, qlo:qhi],
                        start=(i == 0), stop=(i == len(ent) - 1), skip_group_check=True),
                        reads=[constB, PTB[hb * NB + ki]], writes=[bsmB])
                if half == 1:
                    rb = pc % 2
                    p.op("dve", lambda e, bsm=bsm, rb=rb: e.reciprocal(out=rec[:, rb, :], in_=bsm[:, 0:T]),
                         reads=[bsmB], writes=[recB[rb]])
                    p.op("dve", lambda e, bo=bo, rb=rb, pc=pc: e.tensor_tensor(
                        out=oT1[:, pc, :], in0=bo[:, 0:T], in1=rec[:, rb, :], op=ALU.mult),
                        reads=[boB, recB[rb]], writes=[oT1B[pc]])

            for hd in range(min(NBIAS - 1, 16)):
                bias_load(hd)
            attn_a(0)
            for hd in range(16):
                if hd + 1 < 16:
                    attn_a(hd + 1)
                attn_b(hd)
            p.tag = "L1.wout"
            for pj in range(2):
                wp, wB = wout1_pre[pj]
                for q in range(4):
                    ncn = pj * 4 + q
                    bk, bkB = bank()
                    for kc in range(8):
                        mm(bk[:, 0:T], wp[:, kc, q * 128:(q + 1) * 128], oT1[:, kc, :], kc == 0, kc == 7,
                           [wB, oT1B[kc]], [bkB])
                    p.op("dve", lambda e, bk=bk, ncn=ncn: e.tensor_tensor(
                        out=h[:, ncn, :], in0=h[:, ncn, :], in1=bk[:, 0:T], op=ALU.add),
                        reads=[bkB, hB[ncn]], writes=[hB[ncn]])

        def fin_norm():
            p.tag = "fin"
            if final_norm:
                rmsnorm(4, out_fin=True)
            else:
                for c in range(8):
                    p.op("dve", lambda e, c=c: e.tensor_copy(out=fin[:, c, :], in_=h[:, c, :]),
                         reads=[hB[c]], writes=[finB[c]])

        def fin_out(s, t):
            p.tag = "fin"
            t0 = t * T
            for sc in range(NSC):
                ib = sc % 2
                for g in range(2):
                    bk, bkB = bank()
                    for q in range(4):
                        c = g * 4 + q
                        tr(bk[:, q * 128:(q + 1) * 128], fin[:, c, sc * 128:(sc + 1) * 128], identf[:, :],
                           [finB[c], constB], [bkB])
                    if g == 0:
                        p.op("act", lambda e, bk=bk, ib=ib: e.activation(out=io[ib][:, 0:512], in_=bk[:, :], func=AF.Copy),
                             reads=[bkB], writes=[ioB[ib]])
                    else:
                        p.op("dve", lambda e, bk=bk, ib=ib: e.tensor_copy(out=io[ib][:, 512:1024], in_=bk[:, :]),
                             reads=[bkB], writes=[ioB[ib]])
                ev = p.dma("act", out[s, t0 + sc * 128: t0 + (sc + 1) * 128, :], io[ib][:, :], reads=[ioB[ib]])
                out_events.append(ev)

        tiles = [(s, t) for s in range(NSEQ) for t in range(NT)]
        xin_load(*tiles[0])
        xin_tr(*tiles[0])
        wst["inline"] = True
        if stop >= 1:
            layer0_a(*tiles[0])
        for i, (s, t) in enumerate(tiles):
            nxt = tiles[i + 1] if i + 1 < len(tiles) else None
            wst["inline"] = (i == 0)
            if nxt is not None and i > 0:
                xin_load(*nxt)
            if stop >= 1:
                layer0_b(s, t)
            if stop >= 2:
                p.tag = "MLP0"
                mlp(0, 1)
            if stop >= 3:
                layer1(s, t)
            if stop >= 4:
                p.tag = "MLP1"
                mlp(1, 3)
            if nxt is not None and i == 0:
                xin_load(*nxt)
            fin_norm()
            wst["inline"] = False
            if nxt is not None:
                xin_tr(*nxt)
                if stop >= 1:
                    layer0_a(*nxt)
            fin_out(s, t)
        p.emit(final_events=out_events)
        nc._ptags = p.tags
    return nc


def host_consts():
    half = 128
    inv = np.exp(-np.log(10000.0) * np.arange(half, dtype=np.float32) / half).astype(np.float32)
    pos = np.arange(SEQ, dtype=np.float32)
    ang = (pos[:, None] * inv[None, :]).astype(np.float32)
    cosT = np.ascontiguousarray(np.cos(ang.astype(np.float64)).T).astype(np.float32)
    sinT = np.ascontiguousarray(np.sin(ang.astype(np.float64)).T).astype(np.float32)
    lg = np.log1p(-np.exp2(-5.0 - np.arange(4, dtype=np.float64)))
    idx = np.arange(128, dtype=np.float64)
    i = idx[None, :]
    j = idx[:, None]
    allowed = (j <= i) | ((j // 64) == (i // 64))
    maskT = np.zeros((128, 4, 128), np.float32)
    qd = np.zeros((128, 4, 128), np.float32)
    kd = np.zeros((128, 4), np.float32)
    for hh in range(4):
        m = np.exp(lg[hh] * np.abs(i - j)) * allowed * (256 ** -0.5)
        maskT[:, hh, :] = m.astype(np.float32)
        qd[:, hh, :] = np.exp(lg[hh] * (idx + 1.0))[None, :].astype(np.float32)
        kd[:, hh] = (np.exp(lg[hh] * (127.0 - idx)) * (256 ** -0.5)).astype(np.float32)
    identf = np.eye(128, dtype=np.float32)
    return dict(cosT=cosT, sinT=sinT, maskT=maskT, qd=qd, kd=kd, identf=identf)


def bias_table(rel_bias):
    j = np.arange(128)[:, None, None]
    d = np.arange(5)[None, :, None]
    i = np.arange(128)[None, None, :]
    rel = j - i - 128 * d
    bidx = np.maximum(rel, -256) + 256
    valid = (rel <= 63 - (i % 64)) & (rel >= -(i % 64) - 512)
    bidx = np.clip(bidx, 0, 319)
    tab = rel_bias[:, bidx]
    tab = np.where(valid[None], tab, np.float32(-30000.0)).astype(np.float32)
    return np.ascontiguousarray(tab.reshape(16, 128, 640))


def make_in_maps(inputs, ncores, nseq, S):
    hc = host_consts()
    g = np.stack([inputs["mix_norm_g"][0], inputs["mlp_norm_g"][0], inputs["mix_norm_g"][1],
                  inputs["mlp_norm_g"][1], inputs["final_norm_g"]], 0)
    gains = np.ascontiguousarray(g.reshape(5, 8, 128).transpose(2, 0, 1)).astype(np.float32)
    common = dict(
        win0=np.ascontiguousarray(inputs["ret_w_in"][0]), wout0=np.ascontiguousarray(inputs["ret_w_out"][0]),
        w1_0=np.ascontiguousarray(inputs["mlp_w1"][0]), w2_0=np.ascontiguousarray(inputs["mlp_w2"][0]),
        win1=np.ascontiguousarray(inputs["att_w_in"][0]), wout1=np.ascontiguousarray(inputs["att_w_out"][0]),
        w1_1=np.ascontiguousarray(inputs["mlp_w1"][1]), w2_1=np.ascontiguousarray(inputs["mlp_w2"][1]),
        gains=gains, gng=np.ascontiguousarray(inputs["ret_gn_g"][0][None, :]),
        biasT=bias_table(np.asarray(inputs["att_rel_bias"][0], np.float32)), **hc)
    x = np.asarray(inputs["x"], np.float32)
    maps = []
    for c in range(ncores):
        m = dict(common)
        m["x"] = np.ascontiguousarray(x[c * nseq:(c + 1) * nseq, :S, :])
        maps.append(m)
    return maps


_NC_CACHE = {}


def kernel(**inputs):
    inputs = {k: np.asarray(v) for k, v in inputs.items()}
    NSC = 2
    NT = SEQ // (NSC * 128)
    nseq = 16 // NCORES
    key = (nseq, NT, NSC)
    if key not in _NC_CACHE:
        _NC_CACHE[key] = build(nseq, NT, NSC)
    nc = _NC_CACHE[key]
    maps = make_in_maps(inputs, NCORES, nseq, SEQ)
    res = run_bass_kernel_spmd(nc, maps, core_ids=list(range(NCORES)))
    outs = [np.asarray(r["out"]) for r in res.results]
    return np.concatenate(outs, axis=0).astype(np.float32)
```
